# Optimizing a Trainium2 kernel written in Bass

```python
import jax, jax.numpy as jnp
from jax import lax
import numpy as np

D_MODEL = 2048
BATCH = 8
SEQ = 4096
DEPTH = 1
DEC_BATCH = 8
DEC_SEQ = 64
PAST_LEN = 2048

CHUNK = 64
SSD_EXPAND = 2
SSD_D_INNER = SSD_EXPAND * D_MODEL
SSD_HEAD_DIM = 64
SSD_HEADS = SSD_D_INNER // SSD_HEAD_DIM
SSD_GROUPS = 8
SSD_HPG = SSD_HEADS // SSD_GROUPS
SSD_D_STATE = 128
SSD_CONV_W = 4
SSD_CONV_DIM = SSD_D_INNER + 2 * SSD_GROUPS * SSD_D_STATE
SSD_NORM_EPS = 1e-5
GLA_HEADS = 4
GLA_KEY_DIM = D_MODEL // 2
GLA_VAL_DIM = D_MODEL
GLA_HEAD_K = GLA_KEY_DIM // GLA_HEADS
GLA_HEAD_V = GLA_VAL_DIM // GLA_HEADS
GLA_GATE_RANK = 16
GLA_GATE_NORMALIZER = 16.0
N_BRANCHES = 2
NORM_EPS = 1e-6
IN_SPLITS = (SSD_D_INNER, SSD_CONV_DIM, SSD_HEADS, GLA_KEY_DIM, GLA_KEY_DIM, GLA_VAL_DIM, GLA_VAL_DIM, GLA_GATE_RANK, N_BRANCHES * D_MODEL)
IN_COLS = sum(IN_SPLITS)

kernel_name = 'hybrid_ssd_gla_stream_step'


def rmsnorm(x, gain, eps=NORM_EPS):
    x32 = x.astype(jnp.float32)
    y = x32 * lax.rsqrt(jnp.mean(x32 * x32, axis=-1, keepdims=True) + eps)
    return (y * gain.astype(jnp.float32)).astype(x.dtype)


def split_points(sizes):
    return np.cumsum(np.array(sizes))[:-1].tolist()


def causal_mask(length):
    return jnp.tril(jnp.ones((length, length), dtype=bool))


def to_chunks(a, block):
    b, t = a.shape[:2]
    return jnp.moveaxis(a.reshape((b, t // block, block) + a.shape[2:]), 1, 0)


def from_chunks(a):
    nc, b, block = a.shape[:3]
    return jnp.moveaxis(a, 0, 1).reshape((b, nc * block) + a.shape[3:])


def ssd_block(state, xs, dt, bm, cm, a_neg, d_skip):
    length = xs.shape[1]
    s = jnp.cumsum(dt * a_neg, axis=1)
    mask = causal_mask(length)[None, :, :, None, None]
    ldec = jnp.exp(jnp.where(mask, s[:, :, None] - s[:, None, :], -jnp.inf))
    xdt = xs * dt[..., None]
    cb = jnp.einsum('bign,bjgn->bijg', cm, bm)
    y = jnp.einsum('bijg,bijgr,bjgrp->bigrp', cb, ldec, xdt)
    y = y + jnp.einsum('bign,bgrpn->bigrp', cm, state) * jnp.exp(s)[..., None]
    y = y + d_skip[..., None] * xs
    to_end = jnp.exp(s[:, -1:] - s)
    new_state = jnp.exp(s[:, -1])[..., None, None] * state + jnp.einsum('bjgn,bjgr,bjgrp->bgrpn', bm, to_end, xdt)
    return new_state.astype(state.dtype), y.astype(xs.dtype)


def gla_block(state, q, k, v, glog):
    length = q.shape[1]
    bc = jnp.cumsum(glog, axis=1)
    mask = causal_mask(length)[None, :, :, None, None]
    dec = jnp.exp(jnp.where(mask, bc[:, :, None] - bc[:, None, :], -jnp.inf))
    attn = jnp.einsum('bihk,bjhk,bijhk->bhij', q, k, dec)
    o = jnp.einsum('bhij,bjhv->bihv', attn, v)
    o = o + jnp.einsum('bihk,bhkv->bihv', q * jnp.exp(bc), state)
    new_state = jnp.exp(bc[:, -1])[..., None] * state + jnp.einsum('bjhk,bjhv->bhkv', k * jnp.exp(bc[:, -1:] - bc), v)
    return new_state.astype(state.dtype), o.astype(q.dtype)


def hybrid_layer(x, conv_state, ssd_state, gla_state, norm_pre_gain, w_in, conv_w, conv_b, dt_bias, a_log, d_skip,
                 ssd_norm_gain, gla_gk_w, gla_gk_b, gla_norm_gain, w_branch_ssd, w_branch_gla, w_out, norm_post_gain):
    b, t, _ = x.shape
    h = rmsnorm(x, norm_pre_gain)
    proj = h @ w_in
    z_a, xbc, dt_raw, q, k, v, g_b, gk_low, merge = jnp.split(proj, split_points(IN_SPLITS), axis=-1)

    xbc_full = jnp.concatenate([conv_state.astype(xbc.dtype), xbc], axis=1)
    new_conv_state = xbc_full[:, t:]
    conv = conv_b + sum(xbc_full[:, i:i + t] * conv_w[i] for i in range(SSD_CONV_W))
    xbc_c = jax.nn.silu(conv)
    xs, bm, cm = jnp.split(xbc_c, split_points((SSD_D_INNER, SSD_GROUPS * SSD_D_STATE, SSD_GROUPS * SSD_D_STATE)), axis=-1)
    xs = xs.reshape(b, t, SSD_GROUPS, SSD_HPG, SSD_HEAD_DIM)
    bm = bm.reshape(b, t, SSD_GROUPS, SSD_D_STATE)
    cm = cm.reshape(b, t, SSD_GROUPS, SSD_D_STATE)
    dt = jax.nn.softplus(dt_raw.astype(jnp.float32) + dt_bias.astype(jnp.float32)).reshape(b, t, SSD_GROUPS, SSD_HPG)
    a_neg = -jnp.exp(a_log.astype(jnp.float32)).reshape(SSD_GROUPS, SSD_HPG)
    d_sk = d_skip.reshape(SSD_GROUPS, SSD_HPG)

    q = q.reshape(b, t, GLA_HEADS, GLA_HEAD_K) * (GLA_HEAD_K ** -0.5)
    k = k.reshape(b, t, GLA_HEADS, GLA_HEAD_K)
    v = v.reshape(b, t, GLA_HEADS, GLA_HEAD_V)
    glog = jax.nn.log_sigmoid((gk_low @ gla_gk_w + gla_gk_b).astype(jnp.float32)) / GLA_GATE_NORMALIZER
    glog = glog.reshape(b, t, GLA_HEADS, GLA_HEAD_K)

    block = min(CHUNK, t)

    def step(carry, inp):
        s_ssd, s_gla = carry
        xs_c, dt_c, bm_c, cm_c, q_c, k_c, v_c, g_c = inp
        s_ssd, y_ssd_c = ssd_block(s_ssd, xs_c, dt_c, bm_c, cm_c, a_neg, d_sk)
        s_gla, y_gla_c = gla_block(s_gla, q_c, k_c, v_c, g_c)
        return (s_ssd, s_gla), (y_ssd_c, y_gla_c)

    init = (ssd_state.reshape(b, SSD_GROUPS, SSD_HPG, SSD_HEAD_DIM, SSD_D_STATE), gla_state)
    chunks = [to_chunks(a, block) for a in (xs, dt, bm, cm, q, k, v, glog)]
    (new_ssd, new_gla), (y_ssd, y_gla) = lax.scan(step, init, chunks)

    y_ssd = from_chunks(y_ssd).reshape(b, t, SSD_D_INNER) * jax.nn.silu(z_a)
    y_ssd = rmsnorm(y_ssd.reshape(b, t, SSD_GROUPS, -1), ssd_norm_gain.reshape(SSD_GROUPS, -1), SSD_NORM_EPS).reshape(b, t, SSD_D_INNER)
    y_gla = rmsnorm(from_chunks(y_gla), gla_norm_gain) * jax.nn.silu(g_b.reshape(b, t, GLA_HEADS, GLA_HEAD_V))
    y_gla = y_gla.reshape(b, t, GLA_VAL_DIM)

    gate_ssd, gate_gla = jnp.split(jax.nn.sigmoid(merge), 2, axis=-1)
    mixed = gate_ssd * (y_ssd @ w_branch_ssd) + gate_gla * (y_gla @ w_branch_gla)
    y = x + rmsnorm(mixed @ w_out, norm_post_gain)
    return y, new_conv_state, new_ssd.reshape(b, SSD_HEADS, SSD_HEAD_DIM, SSD_D_STATE), new_gla


def setup_inputs(seed: int = 0) -> dict:
    key = jax.random.key(seed)
    ks = jax.random.split(key, 24)
    f32 = jnp.float32
    nrm = lambda k_, shape, scale: jax.random.normal(k_, shape, f32) * scale
    dt0 = jnp.exp(jax.random.uniform(ks[7], (DEPTH, SSD_HEADS), f32, np.log(1e-3), np.log(1e-1)))
    return {
        'x_prompt': nrm(ks[0], (BATCH, SEQ, D_MODEL), 1.0),
        'x_sample': nrm(ks[1], (DEC_BATCH, DEC_SEQ, D_MODEL), 1.0),
        'state_conv_ssd': nrm(ks[2], (DEPTH, DEC_BATCH, SSD_CONV_W - 1, SSD_CONV_DIM), 1.0),
        'state_ssd': nrm(ks[3], (DEPTH, DEC_BATCH, SSD_HEADS, SSD_HEAD_DIM, SSD_D_STATE), 0.3),
        'state_gla': nrm(ks[4], (DEPTH, DEC_BATCH, GLA_HEADS, GLA_HEAD_K, GLA_HEAD_V), 0.5),
        'norm_pre_gain': 1.0 + nrm(ks[5], (DEPTH, D_MODEL), 0.05),
        'w_in': nrm(ks[6], (DEPTH, D_MODEL, IN_COLS), D_MODEL ** -0.5),
        'conv_w': nrm(ks[8], (DEPTH, SSD_CONV_W, SSD_CONV_DIM), SSD_CONV_W ** -0.5),
        'conv_b': nrm(ks[9], (DEPTH, SSD_CONV_DIM), 0.01),
        'dt_bias': dt0 + jnp.log(-jnp.expm1(-dt0)),
        'a_log': jnp.log(jax.random.uniform(ks[10], (DEPTH, SSD_HEADS), f32, 1.0, 16.0)),
        'd_skip': 1.0 + nrm(ks[11], (DEPTH, SSD_HEADS), 0.1),
        'ssd_norm_gain': 1.0 + nrm(ks[12], (DEPTH, SSD_D_INNER), 0.05),
        'gla_gk_w': nrm(ks[13], (DEPTH, GLA_GATE_RANK, GLA_KEY_DIM), GLA_GATE_RANK ** -0.5),
        'gla_gk_b': nrm(ks[14], (DEPTH, GLA_KEY_DIM), 0.1),
        'gla_norm_gain': 1.0 + nrm(ks[15], (DEPTH, GLA_HEAD_V), 0.05),
        'w_branch_ssd': nrm(ks[16], (DEPTH, SSD_D_INNER, D_MODEL), SSD_D_INNER ** -0.5),
        'w_branch_gla': nrm(ks[17], (DEPTH, GLA_VAL_DIM, D_MODEL), GLA_VAL_DIM ** -0.5),
        'w_out': nrm(ks[18], (DEPTH, D_MODEL, D_MODEL), D_MODEL ** -0.5),
        'norm_post_gain': 1.0 + nrm(ks[19], (DEPTH, D_MODEL), 0.05),
    }


def reference(x_prompt, x_sample, state_conv_ssd, state_ssd, state_gla, norm_pre_gain, w_in, conv_w, conv_b, dt_bias,
              a_log, d_skip, ssd_norm_gain, gla_gk_w, gla_gk_b, gla_norm_gain, w_branch_ssd, w_branch_gla, w_out,
              norm_post_gain):
    bp = x_prompt.shape[0]
    dtp = x_prompt.dtype
    yp, ys = x_prompt, x_sample
    conv_p, ssd_p, gla_p, conv_s, ssd_s, gla_s = [], [], [], [], [], []
    for layer in range(DEPTH):
        weights = (norm_pre_gain[layer], w_in[layer], conv_w[layer], conv_b[layer], dt_bias[layer], a_log[layer],
                   d_skip[layer], ssd_norm_gain[layer], gla_gk_w[layer], gla_gk_b[layer], gla_norm_gain[layer],
                   w_branch_ssd[layer], w_branch_gla[layer], w_out[layer], norm_post_gain[layer])
        zc = jnp.zeros((bp, SSD_CONV_W - 1, SSD_CONV_DIM), dtp)
        zs = jnp.zeros((bp, SSD_HEADS, SSD_HEAD_DIM, SSD_D_STATE), dtp)
        zg = jnp.zeros((bp, GLA_HEADS, GLA_HEAD_K, GLA_HEAD_V), dtp)
        yp, c_p, s_p, g_p = hybrid_layer(yp, zc, zs, zg, *weights)
        ys, c_s, s_s, g_s = hybrid_layer(ys, state_conv_ssd[layer], state_ssd[layer], state_gla[layer], *weights)
        conv_p.append(c_p); ssd_p.append(s_p); gla_p.append(g_p)
        conv_s.append(c_s); ssd_s.append(s_s); gla_s.append(g_s)
    return (yp, ys, jnp.stack(conv_p), jnp.stack(ssd_p), jnp.stack(gla_p), jnp.stack(conv_s), jnp.stack(ssd_s), jnp.stack(gla_s))
```

```python
import numpy as np
from contextlib import ExitStack
import concourse.bass as bass
import concourse.mybir as mybir
from concourse.bass_utils import run_bass_kernel_spmd

F32 = mybir.dt.float32
BF16 = mybir.dt.bfloat16
ALU = mybir.AluOpType
AF = mybir.ActivationFunctionType

D = 2048
NCOL = 20560
P_SEQ_FULL = 4096
S_SEQ = 64
C_Z, C_XS, C_B, C_C, C_DT, C_Q, C_K, C_V, C_G, C_GKL, C_M = 0, 4096, 8192, 9216, 10240, 10304, 11328, 12352, 14400, 16448, 16464
K_GPRE, K_GSSD, K_GGLA, K_CW, K_CB, K_GKB, K_DTB, K_ALOG, K_DSK, K_U, K_IDN, K_NEG, K_ONES, K_END = (
    0, 16, 48, 52, 244, 292, 300, 364, 428, 492, 620, 748, 876, 1004)


class Buf:
    __slots__ = ("ap", "w", "r", "dsem", "dcnt", "name")

    def __init__(self, ap, name=""):
        self.ap = ap
        self.w = None
        self.r = {}
        self.dsem = None
        self.dcnt = 0
        self.name = name


class Eng:
    def __init__(self, name, own_deps):
        self.name = name
        self.ops = []
        self.cnt = 0
        self.sem = None
        self.seen = {}
        self.own_deps = own_deps
        self.pending = False


class Sched:
    def __init__(self, nc, es):
        self.nc = nc
        self.es = es
        self.E = {"pe": Eng("pe", False), "act": Eng("act", True), "dve": Eng("dve", True),
                  "pool": Eng("pool", True), "sp": Eng("sp", False)}
        for k, e in self.E.items():
            e.sem = es.enter_context(nc.semaphore("sem_" + k))
        self.nsem = 0
        import os
        self.dry = False
        self.nops = 0
        self.maxops = int(os.environ.get("KMAXOPS", str(10 ** 9)))

    def new_dsem(self):
        self.nsem += 1
        return self.es.enter_context(self.nc.semaphore("dsem%d" % self.nsem))

    def _deps(self, e, reads, writes):
        deps = {}

        def add(tok):
            if tok is None:
                return
            s, v = tok
            k = id(s)
            if k not in deps or deps[k][1] < v:
                deps[k] = (s, v)

        for b in reads:
            add(b.w)
        for b in writes:
            add(b.w)
            for tok in b.r.values():
                add(tok)
        waits = []
        for k, (s, v) in deps.items():
            if s is e.sem and not e.own_deps:
                continue
            if e.seen.get(k, 0) >= v:
                continue
            e.seen[k] = v
            waits.append((s, v))
        return waits

    def op(self, eng, fn, reads=(), writes=(), inc=True):
        if self.dry:
            return None
        self.nops += 1
        if self.nops > self.maxops:
            return None
        if self.maxops < 10 ** 9:
            inc = True
        e = self.E[eng]
        waits = self._deps(e, reads, writes)
        if inc:
            e.cnt += 1
            tok = (e.sem, e.cnt)
            e.pending = False
        else:
            assert not e.own_deps
            tok = (e.sem, e.cnt + 1)
            e.pending = True
        e.ops.append((waits, fn, (e.sem, 1) if inc else None))
        for b in writes:
            b.w = tok
            b.r = {}
        for b in reads:
            b.r[id(e.sem)] = tok
        return tok

    def dma(self, eng, fn, reads=(), writes=(), sem_buf=None):
        if self.dry:
            return None
        self.nops += 1
        if self.nops > self.maxops:
            return None
        e = self.E[eng]
        waits = self._deps(e, reads, writes)
        sb = sem_buf if sem_buf is not None else (writes[0] if writes else reads[0])
        if sb.dsem is None:
            sb.dsem = self.new_dsem()
        sb.dcnt += 16
        tok = (sb.dsem, sb.dcnt)
        e.ops.append((waits, fn, (sb.dsem, 16)))
        for b in writes:
            b.w = tok
            b.r = {}
        for b in reads:
            b.r[id(sb.dsem)] = tok
        return tok

    def wait_all(self, eng, bufs):
        e = self.E[eng]
        waits = self._deps(e, (), bufs)
        e.ops.append((waits, None, None))

    def emit(self):
        nc = self.nc
        hmap = {"pe": "tensor", "act": "scalar", "dve": "vector", "pool": "gpsimd", "sp": "sync"}
        with nc.Block() as block:
            for k, e in self.E.items():
                assert not e.pending, k

                def body(h, e=e):
                    for waits, fn, inc in e.ops:
                        for s, v in waits:
                            h.wait_ge(s, v)
                        if fn is not None:
                            ins = fn(h)
                            if inc is not None:
                                ins.then_inc(inc[0], inc[1])

                getattr(block, hmap[k])(body)


def build_program(P_SEQ, stages=3, dbg=None):
    import os
    dbg = os.environ.get("KDBG", "ia012f") if dbg is None else dbg
    nc = bass.Bass("TRN2", target_bir_lowering=False)

    def din(name, shape):
        return nc.dram_tensor(name, shape, F32, kind="ExternalInput").ap()

    def dout(name, shape):
        return nc.dram_tensor(name, shape, F32, kind="ExternalOutput").ap()

    x_in = {"p": din("xp", [P_SEQ, D]), "s": din("xs", [S_SEQ, D])}
    cs_in = din("cs_in", [3, 6144])
    ss_in = din("ss_in", [4096, 128])
    gs_in = din("gs_in", [4, 256, 512])
    w_in = din("w_in", [D, NCOL])
    wbs = din("wbs", [4096, D])
    wbg = din("wbg", [D, D])
    wo = din("wo", [D, D])
    gkw = din("gkw", [16, 1024])
    cst_d = din("cst", [128, K_END])
    pg_d = din("pgrep", [128, D])
    y_out = {"p": dout("yp", [P_SEQ, D]), "s": dout("ys", [S_SEQ, D])}
    conv_out = {"p": dout("conv_p", [3, 6144]), "s": dout("conv_s", [3, 6144])}
    ssd_out = {"p": dout("ssd_p", [4096, 128]), "s": dout("ssd_s", [4096, 128])}
    gla_out = {"p": dout("gla_p", [4, 256, 512]), "s": dout("gla_s", [4, 256, 512])}

    w_in_v = w_in.rearrange("(kc p) n -> p kc n", p=128)
    wbs_v = wbs.rearrange("(kc p) n -> p kc n", p=128)
    wbg_v = wbg.rearrange("(kc p) n -> p kc n", p=128)
    wo_v = wo.rearrange("(kc p) n -> p kc n", p=128)

    with ExitStack() as es:
        S = Sched(nc, es)

        def sbt(name, shape, dt):
            return es.enter_context(nc.sbuf_tensor("s_" + name, shape, dt))

        cst = Buf(sbt("cst", [128, K_END], F32), "cst")
        cstb_t = sbt("cstb", [128, 256], BF16)
        cstb = Buf(cstb_t, "cstb")
        identb = cstb_t[:, 0:128]
        negb = cstb_t[:, 128:256]
        wdt = Buf(sbt("wdt", [128, 16, 64], BF16), "wdt")
        wgkl = Buf(sbt("wgkl", [128, 16, 16], BF16), "wgkl")
        Wg = Buf(sbt("Wg", [16, 1024], F32), "Wg")
        hT_t = sbt("hT", [128, 16, 512], BF16)
        hT = Buf(hT_t, "hT")
        ysT_t = sbt("ysT", [128, 32, 512], BF16)
        ysT = [Buf(ysT_t[:, 4 * g:4 * g + 4, :], "ysT%d" % g) for g in range(8)]
        ygT_t = sbt("ygT", [128, 16, 512], BF16)
        ygT = [Buf(ygT_t[:, 4 * h:4 * h + 4, :], "ygT%d" % h) for h in range(4)]
        mixT_t = sbt("mixT", [128, 16, 512], BF16)
        mixT = Buf(mixT_t, "mixT")
        ST_t = sbt("ST", [128, 4096], F32)
        ST = [Buf(ST_t[:, 512 * g:512 * g + 512], "ST%d" % g) for g in range(8)]
        Sg_t = sbt("Sg", [128, 8, 512], F32)
        Sg = [Buf(Sg_t[:, k, :], "Sg%d" % k) for k in range(8)]
        NSLOT = 4
        slots = [Buf(sbt("wslot%d" % i, [128, 4096], BF16), "wslot%d" % i) for i in range(NSLOT)]
        hist_t = sbt("hist", [128, 48, 3], F32)
        hist = [Buf(hist_t[:, c, :], "hist%d" % c) for c in range(48)]
        da_all = Buf(sbt("da_all", [128, 256], F32), "da_all")
        lb_all = Buf(sbt("lb_all", [128, 256], F32), "lb_all")
        esh_all = Buf(sbt("esh_all", [128, 256], F32), "esh_all")
        dec_all = Buf(sbt("dec_all", [128, 256], F32), "dec_all")
        w_all = Buf(sbt("w_all", [128, 256], F32), "w_all")
        dt_all = Buf(sbt("dt_all", [128, 256], F32), "dt_all")
        edl_all = Buf(sbt("edl_all", [128, 32], F32), "edl_all")
        gklT = Buf(sbt("gklT", [16, 512], F32), "gklT")
        small = Buf(sbt("small", [128, 64], F32), "small")
        ssm = Buf(sbt("ssm", [128, 64], F32), "ssm")
        NAR = 20
        ar_t = sbt("arena", [128, NAR * 512], F32)
        AR = [Buf(ar_t[:, 512 * i:512 * i + 512], "ar%d" % i) for i in range(NAR)]
        PS = [Buf(es.enter_context(nc.psum_tensor("ps%d" % i, [128, 512], F32))[:, :], "ps%d" % i) for i in range(8)]
        ps_rr = [0]
        ps_mod = [6]

        def ps_next():
            b = PS[ps_rr[0] % ps_mod[0]]
            ps_rr[0] += 1
            assert S.dry or b.w is None or b.r, "PSUM bank %s re-allocated before its consumers were emitted" % b.name
            return b

        class WStream:
            def __init__(self):
                self.plan = []
                self.idx = 0
                self.emitted = 0
                self.record = True
                self.slot_of = {}
                self.free = list(range(NSLOT))

            def _pump(self):
                while self.emitted < len(self.plan) and self.free:
                    sl = self.free.pop(0)
                    key, loader, blk, u = self.plan[self.emitted]
                    slot = slots[sl]
                    if blk == 0 or wscr[0] is None:
                        loader(slot)
                        if wscr[0] is not None:
                            S.dma("sp", lambda h, u=u, slot=slot: h.dma_start(out=wscr[0][u], in_=slot.ap[:, :]),
                                  reads=[slot], writes=[wbB[sl]], sem_buf=wbB[sl])
                    else:
                        S.dma("sp", lambda h, u=u, slot=slot: h.dma_start(out=slot.ap[:, :], in_=wscr[0][u]),
                              reads=wbB, writes=[slot], sem_buf=rbB[sl])
                    self.slot_of[self.emitted] = sl
                    self.emitted += 1

            def get(self, key, loader):
                if self.record:
                    self.plan.append((key, loader, cur_blk[0], cur_u[0]))
                    cur_u[0] += 1
                    return slots[0]
                assert self.plan[self.idx][0] == key, (self.plan[self.idx][0], key)
                if self.idx >= self.emitted:
                    self._pump()
                assert self.idx < self.emitted, ("no free weight slot", key)
                sl = self.slot_of.pop(self.idx)
                self.idx += 1
                return slots[sl]

            def done(self, slot):
                if self.record:
                    return
                self.free.append(slots.index(slot))
                self._pump()

        W = WStream()
        wscr = [None]
        cur_blk = [0]
        cur_u = [0]
        wbB = [Buf(None, "wb%d" % i) for i in range(NSLOT)]
        rbB = [Buf(None, "rb%d" % i) for i in range(NSLOT)]

        def arf(i, n=1):
            return ar_t[:, 512 * i:512 * (i + n)]

        def arb(i, n=1):
            return ar_t[:, 512 * i:512 * (i + n)].bitcast(BF16)

        def arbufs(i, n=1):
            return AR[i:i + n]

        def MM(out, lhsT, rhs, start, stop, reads, writes, inc, skip=False):
            S.op("pe", lambda h: h.matmul(out, lhsT=lhsT, rhs=rhs, start=start, stop=stop, skip_group_check=skip),
                 reads, writes, inc)

        def TR(out, in_, ident, reads, writes, inc):
            S.op("pe", lambda h: h.transpose(out=out, in_=in_, identity=ident), reads, writes, inc)

        def ACT(out, in_, func, reads, writes, bias=None, scale=None, accum=None):
            kw = {}
            if bias is not None:
                kw["bias"] = bias
            if scale is not None:
                kw["scale"] = scale
            if accum is not None:
                kw["accum_out"] = accum
            S.op("act", lambda h: h.activation(out=out, in_=in_, func=func, **kw), reads, writes)

        def TS(out, in0, s1, s2, op0, op1, reads, writes, eng="dve"):
            if op1 is None:
                S.op(eng, lambda h: h.tensor_scalar(out=out, in0=in0, scalar1=s1, scalar2=None, op0=op0), reads, writes)
            else:
                S.op(eng, lambda h: h.tensor_scalar(out=out, in0=in0, scalar1=s1, scalar2=s2, op0=op0, op1=op1), reads, writes)

        def TT(out, in0, in1, op, reads, writes, eng="dve"):
            S.op(eng, lambda h: h.tensor_tensor(out=out, in0=in0, in1=in1, op=op), reads, writes)

        def STT(out, in0, scalar, in1, op0, op1, reads, writes):
            S.op("dve", lambda h: h.scalar_tensor_tensor(out=out, in0=in0, scalar=scalar, in1=in1, op0=op0, op1=op1), reads, writes)

        def CP(out, in_, reads, writes, eng="dve"):
            if eng == "act":
                S.op("act", lambda h: h.activation(out=out, in_=in_, func=AF.Copy), reads, writes)
            else:
                S.op(eng, lambda h: h.tensor_copy(out=out, in_=in_), reads, writes)

        def MEMSET(out, val, writes, eng="dve"):
            S.op(eng, lambda h: h.memset(out, val), (), writes)

        def DMA(q, out, in_, reads, writes, sem_buf=None):
            S.dma(q, lambda h: h.dma_start(out=out, in_=in_), reads, writes, sem_buf)

        def RSTD(out_col, ss_col, mult, eps, reads_writes, eng="dve"):
            n = ss_col.shape[0]
            TS(out_col, ss_col, mult, eps, ALU.mult, ALU.add, reads_writes, reads_writes, eng=eng)
            mh = ssm if (len(reads_writes) == 1 and reads_writes[0] is ssm) else small
            TT(out_col, out_col, mh.ap[0:n, 63:64], ALU.pow, reads_writes + ([small] if mh is small else []), reads_writes, eng="pool")

        def SOFTPLUS(z, xa, y, tmp, bufs):
            ACT(y, xa, AF.Exp, bufs, bufs)
            TS(y, y, 1.0, None, ALU.add, None, bufs, bufs)
            TS(z, xa, 0.0, 0.35, ALU.max, ALU.add, bufs, bufs)
            for _ in range(4):
                ACT(tmp, z, AF.Exp, bufs, bufs, scale=-1.0)
                TT(tmp, tmp, y, ALU.mult, bufs, bufs)
                STT(z, z, -1.0, tmp, ALU.add, ALU.add, bufs, bufs)

        c = cst.ap
        U = c[:, K_U:K_U + 128]
        IDN = c[:, K_IDN:K_IDN + 128]
        ONES = c[:, K_ONES:K_ONES + 128]

        DMA("sp", cst.ap[:], cst_d[:, :], [], [cst])
        DMA("sp", Wg.ap[:], gkw[:, :], [], [Wg])
        DMA("pool", wdt.ap[:], w_in_v[:, :, C_DT:C_DT + 64], [], [wdt])
        DMA("pool", wgkl.ap[:], w_in_v[:, :, C_GKL:C_GKL + 16], [], [wgkl])
        MEMSET(small.ap[:, 63:64], -0.5, [small])
        MEMSET(ssm.ap[:, 63:64], -0.5, [ssm])
        TS(c[:, K_CW:K_GKB], c[:, K_CW:K_GKB], 0.5, None, ALU.mult, None, [cst], [cst])
        TS(c[:, K_GKB:K_DTB], c[:, K_GKB:K_DTB], -1.0, None, ALU.mult, None, [cst], [cst])
        ACT(c[:, K_ALOG:K_DSK], c[:, K_ALOG:K_DSK], AF.Exp, [cst], [cst])
        TS(c[:, K_ALOG:K_DSK], c[:, K_ALOG:K_DSK], -1.0, None, ALU.mult, None, [cst], [cst])
        CP(cstb_t[:, 0:128], c[:, K_IDN:K_IDN + 128], [cst], [cstb])
        CP(cstb_t[:, 128:256], c[:, K_NEG:K_NEG + 128], [cst], [cstb])

        def load_w(dst_ap, src_ap, slot):
            DMA("pool", dst_ap, src_ap, [], [slot], sem_buf=slot)

        def phase_a(seq, t0, nt, tn):
            for t in range(nt):
                xt = ysT_t[:, 8 * t:8 * t + 8, :].rearrange("p a b -> p (a b)").bitcast(F32)
                xtb = ysT[2 * t:2 * t + 2]
                xn = ygT_t[:, 4 * t:4 * t + 4, :].rearrange("p a b -> p (a b)")
                junk = mixT_t[:, 0:4, :].rearrange("p a b -> p (a b)")
                r0 = t0 + t * tn
                DMA("sp", xt[0:tn, :], x_in[seq][r0:r0 + tn, :], [], xtb, sem_buf=xtb[0])
                ACT(junk[0:tn, :], xt[0:tn, :], AF.Square, xtb, [mixT, small], accum=small.ap[0:tn, 0:1])
                RSTD(small.ap[0:tn, 1:2], small.ap[0:tn, 0:1], 1.0 / D, 1e-6, [small])
                TS(xn[0:tn, :], xt[0:tn, :], small.ap[0:tn, 1:2], None, ALU.mult, None, xtb + [small], [ygT[t]])
                for half in range(2):
                    pb = ps_next()
                    pv = pb.ap[:].bitcast(BF16)
                    for k8 in range(8):
                        kc = half * 8 + k8
                        TR(pv[:, k8 * tn:(k8 + 1) * tn], xn[0:tn, kc * 128:(kc + 1) * 128], identb[0:tn, 0:tn],
                           [ygT[t], cstb], [pb], inc=(k8 == 7))
                    TT(hT_t[:, half * 8:half * 8 + 8, t * tn:(t + 1) * tn],
                       pv[:, 0:8 * tn].rearrange("p (a b) -> p a b", a=8),
                       c[:, K_GPRE + half * 8:K_GPRE + half * 8 + 8].unsqueeze(2).broadcast_to([128, 8, tn]),
                       ALU.mult, [pb, cst], [hT])

        def phase_b0(nt, tn):
            Wd = nt * 64
            a0 = arf(0)
            a1 = arf(11)
            xa, yy = a0[0:tn, 0:Wd], a0[0:tn, 256:256 + Wd]
            tmp = a1[0:tn, 0:Wd]
            scr = a1[:, 256:256 + Wd]
            bb = [AR[0], AR[11]]
            v3 = lambda ap_: ap_.rearrange("p (a b) -> p a b", b=64)
            pb = ps_next()
            for t in range(nt):
                cols = slice(t * tn, (t + 1) * tn)
                for kc in range(16):
                    MM(pb.ap[0:tn, t * 64:(t + 1) * 64], hT_t[:, kc, cols], wdt.ap[:, kc, :], kc == 0, kc == 15, [hT, wdt], [pb], kc == 15, skip=True)
            TT(v3(xa), v3(pb.ap[0:tn, 0:Wd]), c[0:tn, K_DTB:K_DTB + 64].unsqueeze(1).broadcast_to([tn, nt, 64]), ALU.add, [pb, cst], bb)
            SOFTPLUS(dt_all.ap[0:tn, 0:Wd], xa, yy, tmp, bb + [dt_all])
            TT(v3(da_all.ap[0:tn, 0:Wd]), v3(dt_all.ap[0:tn, 0:Wd]), c[0:tn, K_ALOG:K_ALOG + 64].unsqueeze(1).broadcast_to([tn, nt, 64]), ALU.mult,
               [dt_all, cst], [da_all])
            pb2 = ps_next()
            for t in range(nt):
                MM(pb2.ap[0:tn, t * 64:(t + 1) * 64], U[0:tn, 0:tn], da_all.ap[0:tn, t * 64:(t + 1) * 64], True, True, [cst, da_all], [pb2], True, skip=True)
                MM(pb2.ap[:, 256 + t * 64:256 + (t + 1) * 64], ONES[0:tn, 0:128], da_all.ap[0:tn, t * 64:(t + 1) * 64], True, True, [cst, da_all], [pb2], True, skip=True)
            TS(lb_all.ap[0:tn, 0:Wd], pb2.ap[0:tn, 0:Wd], -1.0, None, ALU.mult, None, [pb2], [lb_all])
            ACT(esh_all.ap[0:tn, 0:Wd], lb_all.ap[0:tn, 0:Wd], AF.Exp, [lb_all], [esh_all], scale=-1.0)
            TS(esh_all.ap[0:tn, 0:Wd], esh_all.ap[0:tn, 0:Wd], 0.5, None, ALU.mult, None, [esh_all], [esh_all])
            CP(scr, pb2.ap[:, 256:256 + Wd], [pb2], bb)
            ACT(dec_all.ap[:, 0:Wd], scr, AF.Exp, bb, [dec_all])
            TT(tmp, lb_all.ap[0:tn, 0:Wd], pb2.ap[0:tn, 256:256 + Wd], ALU.add, [lb_all, pb2], bb)
            ACT(tmp, tmp, AF.Exp, bb, bb)
            TT(w_all.ap[0:tn, 0:Wd], tmp, dt_all.ap[0:tn, 0:Wd], ALU.mult, bb + [dt_all], [w_all])

        st1 = {}

        def ssd_stage1_chunk(g, ci, nt, tn, par):
            TB = nt * tn
            base = 1 + par * 5
            xsT = arb(base, 2).rearrange("p (a b) -> p a b", a=4)
            BT = arf(base + 2)
            CT = arf(base + 3)
            BTb = arb(base + 4)[:, 0:512]
            gb = arbufs(base, 5)
            if ci == 0:
                st1["A"] = W.get(("xsA", g), lambda s_, g=g: load_w(s_.ap[:].rearrange("p (k n) -> p k n", k=16),
                                                                   w_in_v[:, :, C_XS + 512 * g:C_XS + 512 * g + 256], s_))
            if ci == 2:
                st1["B"] = W.get(("xsB", g), lambda s_, g=g: load_w(s_.ap[:].rearrange("p (k n) -> p k n", k=16),
                                                                   w_in_v[:, :, C_XS + 512 * g + 256:C_XS + 512 * g + 512], s_))
            if ci == 4:
                def ldc(s_, g=g):
                    v = s_.ap[:].rearrange("p (k n) -> p k n", k=16)
                    load_w(v[:, :, 0:128], w_in_v[:, :, C_B + 128 * g:C_B + 128 * g + 128], s_)
                    load_w(v[:, :, 128:256], w_in_v[:, :, C_C + 128 * g:C_C + 128 * g + 128], s_)
                st1["C"] = W.get(("BC", g), ldc)
            if ci < 2:
                sl, co, cg = st1["A"], ci * 128, 4 * g + ci
            elif ci < 4:
                sl, co, cg = st1["B"], (ci - 2) * 128, 4 * g + ci
            elif ci == 4:
                sl, co, cg = st1["C"], 0, 32 + g
            else:
                sl, co, cg = st1["C"], 128, 40 + g
            sv = sl.ap[:].rearrange("p (k n) -> p k n", k=16)
            if True:
                pb = ps_next()
                for kc in range(16):
                    MM(pb.ap[:, 0:TB], sv[:, kc, co:co + 128], hT_t[:, kc, 0:TB], kc == 0, kc == 15, [sl, hT], [pb], kc == 15)
                if ci in (1, 3, 5):
                    W.done(sl)
                acc_u, th_u = 13, 14
                acc = arf(acc_u)
                th = arf(th_u)
                ab = [AR[acc_u], AR[th_u]]
                w = lambda i: c[:, K_CW + cg * 4 + i:K_CW + cg * 4 + i + 1]
                X = pb.ap
                ACT(acc[:, 0:TB], X[:, 0:TB], AF.Identity, [pb, cst], ab, bias=c[:, K_CB + cg:K_CB + cg + 1], scale=w(3))
                ACT(th[:, 0:TB], X[:, 0:TB], AF.Copy, [pb], ab)
                Xc = th
                for sh, wi in ((1, 2), (2, 1), (3, 0)):
                    STT(acc[:, sh:TB], Xc[:, 0:TB - sh], w(wi), acc[:, sh:TB], ALU.mult, ALU.add, [cst] + ab, ab)
                hc = hist[cg]
                STT(acc[:, 0:3], hc.ap[:, 0:3], w(0), acc[:, 0:3], ALU.mult, ALU.add, [hc, cst] + ab, ab)
                STT(acc[:, 0:2], hc.ap[:, 1:3], w(1), acc[:, 0:2], ALU.mult, ALU.add, [hc, cst] + ab, ab)
                STT(acc[:, 0:1], hc.ap[:, 2:3], w(2), acc[:, 0:1], ALU.mult, ALU.add, [hc, cst] + ab, ab)
                CP(hist_t[:, cg, :], Xc[:, TB - 3:TB], ab, [hc])
                ACT(th[:, 0:TB], acc[:, 0:TB], AF.Tanh, ab, ab)
                if ci < 4:
                    dst = xsT[:, ci, 0:TB]
                elif ci == 4:
                    dst = BT[:, 0:TB]
                else:
                    dst = CT[:, 0:TB]
                STT(dst, th[:, 0:TB], 1.0, acc[:, 0:TB], ALU.add, ALU.mult, ab, gb)
                if ci == 4:
                    CP(BTb[:, 0:TB], BT[:, 0:TB], gb, gb, eng="act")

        def ssd_stage2(g, nt, tn, par, hook=None, carry=None):
            base = 1 + par * 5
            xsT = arb(base, 2).rearrange("p (a b) -> p a b", a=4)
            BT = arf(base + 2)
            CT = arf(base + 3)
            BTb = arb(base + 4)[:, 0:512]
            gb = arbufs(base, 5)
            xs_tok = arb(15)[:, 0:512]
            xsd = arb(15)[:, 512:1024]
            xw = arb(16)[:, 0:512]
            Btok = arb(16)[:, 512:640]
            cbm = arb(16)[:, 640:768]
            tb1 = arbufs(15, 2)
            LT = arb(17)[:, :].rearrange("p (a b) -> p a b", a=8)
            MT = arb(18)[:, :].rearrange("p (a b) -> p a b", a=8)
            tb2 = arbufs(17, 2)
            y1sb = arf(19)
            tb3 = arbufs(19)
            yv = arf(0)
            tz = arf(11)
            junkf = arf(19)
            tb4 = arbufs(0)
            tb5 = arbufs(11)
            tb6 = arbufs(12)
            pending = [carry]
            chain_pending = [None]
            ps_mod[0] = 4
            sZ = [W.get(("Z", g, i), lambda s_, g=g, i=i: load_w(s_.ap[:].rearrange("p (k n) -> p k n", k=16),
                                                               w_in_v[:, :, C_Z + 512 * g + 256 * i:C_Z + 512 * g + 256 * i + 256], s_))
                  for i in range(2)]
            sZv = [s_.ap[:].rearrange("p (k n) -> p k n", k=16) for s_ in sZ]
            for t in range(nt):
                cols = slice(t * tn, (t + 1) * tn)
                p1 = ps_next()
                p1v = p1.ap[:].bitcast(BF16)
                for ci in range(4):
                    TR(p1v[0:tn, ci * 128:(ci + 1) * 128], xsT[:, ci, cols], identb, gb + [cstb], [p1], ci == 3)
                TR(p1v[0:tn, 512:640], BTb[:, cols], identb, gb + [cstb], [p1], True)
                CP(Btok[0:tn, :], p1v[0:tn, 512:640], [p1], tb1, eng="act")
                p13 = p1v[0:tn, 0:512].rearrange("p (a b) -> p a b", a=8)
                TT(xs_tok[0:tn, :].rearrange("p (a b) -> p a b", a=8), p13,
                   dt_all.ap[0:tn, t * 64 + 8 * g:t * 64 + 8 * g + 8].unsqueeze(2).broadcast_to([tn, 8, 64]), ALU.mult, [p1, dt_all], tb1)
                TT(xsd[0:tn, :].rearrange("p (a b) -> p a b", a=8), p13,
                   c[0:tn, K_DSK + 8 * g:K_DSK + 8 * g + 8].unsqueeze(2).broadcast_to([tn, 8, 64]), ALU.mult, [p1, cst], tb1)
                TT(xw[0:tn, :].rearrange("p (a b) -> p a b", a=8), p13,
                   w_all.ap[0:tn, t * 64 + 8 * g:t * 64 + 8 * g + 8].unsqueeze(2).broadcast_to([tn, 8, 64]), ALU.mult, [p1, w_all], tb1)
                p2 = ps_next()
                MM(p2.ap[0:tn, 0:tn], BT[:, cols], CT[:, cols], True, True, gb, [p2], True)
                TT(cbm[0:tn, 0:tn], p2.ap[0:tn, 0:tn], U[0:tn, 0:tn], ALU.mult, [p2, cst], tb1)
                for half in range(2):
                    p3 = ps_next()
                    for h4 in range(4):
                        hh = half * 4 + h4
                        h = 8 * g + hh
                        o = p3.ap[0:tn, h4 * 128:h4 * 128 + tn]
                        MM(o, da_all.ap[0:tn, t * 64 + h:t * 64 + h + 1].broadcast_to([tn, tn]), U[0:tn, 0:tn], True, False, [da_all, cst], [p3], False, skip=True)
                        MM(o, identb[0:tn, 0:tn], negb[0:tn, 0:tn], False, True, [cstb], [p3], h4 == 3, skip=True)
                    for h4 in range(4):
                        hh = half * 4 + h4
                        h = 8 * g + hh
                        ACT(arb(17)[0:tn, hh * 128:hh * 128 + tn], p3.ap[0:tn, h4 * 128:h4 * 128 + tn], AF.Exp, [p3, lb_all], tb2,
                            bias=lb_all.ap[0:tn, t * 64 + h:t * 64 + h + 1], scale=1.0)
                TT(MT[0:tn, :, 0:tn], LT[0:tn, :, 0:tn], cbm[0:tn, 0:tn].unsqueeze(1).broadcast_to([tn, 8, tn]), ALU.mult, tb2 + tb1, tb2)
                if chain_pending[0] is not None:
                    chain_pending[0]()
                    chain_pending[0] = None
                p8 = PS[6 + t % 2]
                MM(p8.ap[:, :], Btok[0:tn, :], xw[0:tn, :], True, True, tb1, [p8], True)
                p6 = ps_next()
                for i in range(2):
                    for kc in range(16):
                        MM(p6.ap[0:tn, 256 * i:256 * i + 256], hT_t[:, kc, cols], sZv[i][:, kc, :], kc == 0, kc == 15, [hT, sZ[i]], [p6],
                           kc == 15, skip=True)
                if t == nt - 1:
                    W.done(sZ[0])
                    W.done(sZ[1])
                ACT(tz[0:tn, :], p6.ap[0:tn, :], AF.Tanh, [p6], tb5, scale=0.5)
                STT(tz[0:tn, :], tz[0:tn, :], 1.0, p6.ap[0:tn, :], ALU.add, ALU.mult, [p6] + tb5, tb5)
                if pending[0] is not None:
                    if hook is not None and t > 0:
                        hook(t - 1)
                    pending[0]()
                    pending[0] = None
                p4 = PS[4]
                MM(p4.ap[0:tn, :], identb[0:tn, 0:tn], xsd[0:tn, :], True, False, [cstb] + tb1, [p4], False, skip=True)
                for hh in range(8):
                    MM(p4.ap[0:tn, hh * 64:(hh + 1) * 64], MT[0:tn, hh, 0:tn], xs_tok[0:tn, hh * 64:(hh + 1) * 64], False, hh == 7,
                       tb2 + tb1, [p4], hh == 7, skip=True)
                p5 = PS[5]
                MM(p5.ap[0:tn, :], CT[:, cols], ST[g].ap[:, :], True, True, gb + [ST[g]], [p5], True)

                def chain(t=t, cols=cols, p4=p4, p5=p5, p8=p8):
                    TT(y1sb[0:tn, :].rearrange("p (a b) -> p a b", a=8), p5.ap[0:tn, :].rearrange("p (a b) -> p a b", a=8),
                       esh_all.ap[0:tn, t * 64 + 8 * g:t * 64 + 8 * g + 8].unsqueeze(2).broadcast_to([tn, 8, 64]), ALU.mult, [p5, esh_all], tb3)
                    STT(yv[0:tn, :], p4.ap[0:tn, :], 0.5, y1sb[0:tn, :], ALU.mult, ALU.add, [p4] + tb3, tb4)
                    st3 = ST[g].ap[:, :].rearrange("p (a b) -> p a b", a=8)
                    TT(st3, st3, dec_all.ap[:, t * 64 + 8 * g:t * 64 + 8 * g + 8].unsqueeze(2).broadcast_to([128, 8, 64]), ALU.mult,
                       [ST[g], dec_all], [ST[g]], eng="pool")
                    TT(ST[g].ap[:, :], ST[g].ap[:, :], p8.ap[:, :], ALU.add, [ST[g], p8], [ST[g]])
                    TT(yv[0:tn, :], yv[0:tn, :], tz[0:tn, :], ALU.mult, tb4 + tb5, tb4, eng="pool")
                    ACT(junkf[0:tn, :], yv[0:tn, :], AF.Square, tb4, tb3 + [ssm], accum=ssm.ap[0:tn, 2:3])
                    RSTD(ssm.ap[0:tn, 3:4], ssm.ap[0:tn, 2:3], 1.0 / 512, 1e-5, [ssm], eng="pool")
                    yn = arb(12)[:, (t % 2) * 512:(t % 2) * 512 + 512]
                    TS(yn[0:tn, :], yv[0:tn, :], ssm.ap[0:tn, 3:4], 0.0, ALU.mult, ALU.add, tb4 + [ssm], tb6, eng="pool")

                    def ytrans():
                        p7 = ps_next()
                        p7v = p7.ap[:].bitcast(BF16)
                        for ci in range(4):
                            TR(p7v[:, ci * tn:(ci + 1) * tn], yn[0:tn, ci * 128:(ci + 1) * 128], identb[0:tn, 0:tn], tb6 + [cstb], [p7], ci == 3)
                        TT(ysT_t[:, 4 * g:4 * g + 4, cols], p7v[:, 0:4 * tn].rearrange("p (a b) -> p a b", a=4),
                           c[:, K_GSSD + 4 * g:K_GSSD + 4 * g + 4].unsqueeze(2).broadcast_to([128, 4, tn]), ALU.mult, [p7, cst], [ysT[g]])
                    pending[0] = ytrans
                chain_pending[0] = chain
            chain_pending[0]()
            if hook is not None:
                hook(nt - 1)
            ps_mod[0] = 6
            return pending[0]

        def phase_b(nt, tn):
            if "0" in dbg:
                phase_b0(nt, tn)
            if "1" not in dbg:
                return
            for ci in range(6):
                ssd_stage1_chunk(0, ci, nt, tn, 0)
            carry = None
            for g in range(8):
                def hook(t, g=g):
                    if g + 1 < 8:
                        for ci in range(6):
                            if min(ci // 2, nt - 1) == t:
                                ssd_stage1_chunk(g + 1, ci, nt, tn, (g + 1) % 2)
                if "2" in dbg:
                    carry = ssd_stage2(g, nt, tn, g % 2, hook, carry)
                else:
                    for t in range(nt):
                        hook(t)
            if carry is not None:
                ps_mod[0] = 4
                carry()
                ps_mod[0] = 6

        def phase_c(nt, tn):
            TB = nt * tn
            pb = ps_next()
            for kc in range(16):
                MM(pb.ap[0:16, 0:TB], wgkl.ap[:, kc, :], hT_t[:, kc, 0:TB], kc == 0, kc == 15, [wgkl, hT], [pb], kc == 15)
            CP(gklT.ap[0:16, 0:TB], pb.ap[0:16, 0:TB], [pb], [gklT], eng="act")
            qpT = arf(1, 2).rearrange("p (a b) -> p a b", a=2)
            kpT = arf(3, 2).rearrange("p (a b) -> p a b", a=2)
            qb = arbufs(1, 2)
            kb = arbufs(3, 2)
            ebuf, lbuf, clbuf, eqb = arf(5), arf(6), arf(7), arf(8)
            eb = arbufs(5, 4)
            kpp_pending = [None]
            kpp_tok = arb(10)[:, :].rearrange("p (a b) -> p a b", a=4)
            kb2 = arbufs(9, 2)
            v_tok = arb(15)[:, 0:512]
            onb = arb(15)[:, 512:1024]
            am = arb(16)[:, 0:128]
            junk = arb(16)[:, 512:1024]
            tg = arf(17)
            vb = arbufs(15, 3)
            for hd in range(4):
                sQK = W.get(("Q", hd), lambda s_, hd=hd: load_w(s_.ap[:].rearrange("p (k n) -> p k n", k=16),
                                                               w_in_v[:, :, C_Q + 256 * hd:C_Q + 256 * hd + 256], s_))
                sQKv = sQK.ap[:].rearrange("p (k n) -> p k n", k=16)
                sKK = W.get(("K", hd), lambda s_, hd=hd: load_w(s_.ap[:].rearrange("p (k n) -> p k n", k=16),
                                                               w_in_v[:, :, C_K + 256 * hd:C_K + 256 * hd + 256], s_))
                sKKv = sKK.ap[:].rearrange("p (k n) -> p k n", k=16)
                for kc2 in range(2):
                    kk = 2 * hd + kc2
                    p1 = ps_next()
                    MM(p1.ap[:, 0:TB], Wg.ap[0:16, kk * 128:(kk + 1) * 128], gklT.ap[0:16, 0:TB], True, True, [Wg, gklT], [p1], True)
                    TS(ebuf[:, 0:TB], p1.ap[:, 0:TB], -1.0, c[:, K_GKB + kk:K_GKB + kk + 1], ALU.mult, ALU.add, [p1, cst], eb)
                    ACT(eqb[:, 0:TB], ebuf[:, 0:TB], AF.Exp, eb, eb)
                    ACT(lbuf[:, 0:TB], eqb[:, 0:TB], AF.Ln, eb, eb, bias=1.0)
                    for t in range(nt):
                        cols = slice(t * tn, (t + 1) * tn)
                        S.op("dve", lambda h, cols=cols: h.tensor_tensor_scan(out=clbuf[:, cols], data0=ONES[:, 0:tn], data1=lbuf[:, cols],
                                                                             initial=0.0, op0=ALU.mult, op1=ALU.add), eb + [cst], eb)
                        ACT(edl_all.ap[:, kk * 4 + t:kk * 4 + t + 1], clbuf[:, (t + 1) * tn - 1:(t + 1) * tn], AF.Exp, eb, [edl_all], scale=-1.0 / 16)
                    p2 = ps_next()
                    for kc in range(16):
                        MM(p2.ap[:, 0:TB], sQKv[:, kc, kc2 * 128:(kc2 + 1) * 128], hT_t[:, kc, 0:TB], kc == 0, kc == 15, [sQK, hT], [p2], kc == 15)
                    ACT(eqb[:, 0:TB], clbuf[:, 0:TB], AF.Exp, eb, eb, scale=-1.0 / 16)
                    STT(qpT[:, kc2, 0:TB], p2.ap[:, 0:TB], 1.0 / 16, eqb[:, 0:TB], ALU.mult, ALU.mult, [p2] + eb, qb)
                    p3 = ps_next()
                    for kc in range(16):
                        MM(p3.ap[:, 0:TB], sKKv[:, kc, kc2 * 128:(kc2 + 1) * 128], hT_t[:, kc, 0:TB], kc == 0, kc == 15, [sKK, hT], [p3], kc == 15)
                    if kc2 == 1:
                        W.done(sQK)
                        W.done(sKK)
                        kpp_pending[0]()
                    ACT(eqb[:, 0:TB], clbuf[:, 0:TB], AF.Exp, eb, eb, scale=1.0 / 16)
                    TT(kpT[:, kc2, 0:TB], p3.ap[:, 0:TB], eqb[:, 0:TB], ALU.mult, [p3] + eb, kb)
                    for t in range(nt):
                        cols = slice(t * tn, (t + 1) * tn)
                        TS(arb(9)[:, (kc2 * 4 + t) * 128:(kc2 * 4 + t) * 128 + tn], kpT[:, kc2, cols], edl_all.ap[:, kk * 4 + t:kk * 4 + t + 1],
                           None, ALU.mult, None, kb + [edl_all], [AR[9]])

                    def kpp_tr(kc2=kc2):
                        for t in range(nt):
                            p4 = ps_next()
                            p4v = p4.ap[:].bitcast(BF16)
                            TR(p4v[0:tn, 0:128], arb(9)[:, (kc2 * 4 + t) * 128:(kc2 * 4 + t) * 128 + tn], identb, [AR[9], cstb], [p4], True)
                            CP(arb(10)[0:tn, t * 256 + kc2 * 128:t * 256 + (kc2 + 1) * 128], p4v[0:tn, 0:128], [p4], [AR[10]], eng="act")
                    kpp_pending[0] = kpp_tr
                sV = [W.get(("V", hd, i), lambda s_, hd=hd, i=i: load_w(s_.ap[:].rearrange("p (k n) -> p k n", k=16),
                                                                      w_in_v[:, :, C_V + 512 * hd + 256 * i:C_V + 512 * hd + 256 * i + 256], s_))
                      for i in range(2)]
                sVv = [s_.ap[:].rearrange("p (k n) -> p k n", k=16) for s_ in sV]
                for t in range(nt):
                    cols = slice(t * tn, (t + 1) * tn)
                    p5 = ps_next()
                    for i in range(2):
                        for kc in range(16):
                            MM(p5.ap[0:tn, 256 * i:256 * i + 256], hT_t[:, kc, cols], sVv[i][:, kc, :], kc == 0, kc == 15, [hT, sV[i]], [p5],
                               kc == 15, skip=True)
                    vt = arb(11 + t // 2)[:, (t % 2) * 512:(t % 2) * 512 + 512]
                    CP(vt[0:tn, :], p5.ap[0:tn, :], [p5], arbufs(11 + t // 2), eng="act")
                    if t == nt - 1:
                        W.done(sV[0])
                        W.done(sV[1])
                kpp_pending[0]()
                sG = [W.get(("G", hd, i), lambda s_, hd=hd, i=i: load_w(s_.ap[:].rearrange("p (k n) -> p k n", k=16),
                                                                      w_in_v[:, :, C_G + 512 * hd + 256 * i:C_G + 512 * hd + 256 * i + 256], s_))
                      for i in range(2)]
                sGv = [s_.ap[:].rearrange("p (k n) -> p k n", k=16) for s_ in sG]
                onB, amB, jkB = arbufs(15), arbufs(16), arbufs(19)
                junkf = arf(19)

                def gla_G(t):
                    cols = slice(t * tn, (t + 1) * tn)
                    tgt = arf(17 + t % 2)
                    tgB = arbufs(17 + t % 2)
                    p6 = ps_next()
                    for i in range(2):
                        for kc in range(16):
                            MM(p6.ap[0:tn, 256 * i:256 * i + 256], hT_t[:, kc, cols], sGv[i][:, kc, :], kc == 0, kc == 15, [hT, sG[i]], [p6],
                               kc == 15, skip=True)
                    if t == nt - 1:
                        W.done(sG[0])
                        W.done(sG[1])
                    ACT(tgt[0:tn, :], p6.ap[0:tn, :], AF.Tanh, [p6], tgB, scale=0.5)
                    STT(tgt[0:tn, :], tgt[0:tn, :], 1.0, p6.ap[0:tn, :], ALU.add, ALU.mult, [p6] + tgB, tgB)

                def gla_attn(t):
                    cols = slice(t * tn, (t + 1) * tn)
                    p7 = ps_next()
                    for kc2 in range(2):
                        MM(p7.ap[0:tn, 0:tn], kpT[:, kc2, cols], qpT[:, kc2, cols], kc2 == 0, kc2 == 1, kb + qb, [p7], kc2 == 1)
                    TT(am[0:tn, 0:tn], p7.ap[0:tn, 0:tn], U[0:tn, 0:tn], ALU.mult, [p7, cst], amB)

                def gla_O(t):
                    cols = slice(t * tn, (t + 1) * tn)
                    vt = arb(11 + t // 2)[:, (t % 2) * 512:(t % 2) * 512 + 512]
                    vtb = arbufs(11 + t // 2)
                    tgt = arf(17 + t % 2)
                    tgB = arbufs(17 + t % 2)
                    p8 = ps_next()
                    MM(p8.ap[0:tn, :], am[0:tn, 0:tn], vt[0:tn, :], True, False, amB + vtb, [p8], False, skip=True)
                    for kc2 in range(2):
                        kk = 2 * hd + kc2
                        MM(p8.ap[0:tn, :], qpT[:, kc2, cols], Sg[kk].ap[:, :], False, kc2 == 1, qb + [Sg[kk]], [p8], kc2 == 1, skip=True)
                    ACT(junkf[0:tn, :], p8.ap[0:tn, :], AF.Square, [p8], jkB + [small], accum=small.ap[0:tn, 4:5])
                    RSTD(small.ap[0:tn, 5:6], small.ap[0:tn, 4:5], 4.0 / 512, 4e-6, [small])
                    STT(onb[0:tn, :], p8.ap[0:tn, :], small.ap[0:tn, 5:6], tgt[0:tn, :], ALU.mult, ALU.mult, [p8, small] + tgB, onB)

                def gla_tail(t):
                    cols = slice(t * tn, (t + 1) * tn)
                    vt = arb(11 + t // 2)[:, (t % 2) * 512:(t % 2) * 512 + 512]
                    vtb = arbufs(11 + t // 2)
                    p9 = ps_next()
                    p9v = p9.ap[:].bitcast(BF16)
                    for ci in range(4):
                        TR(p9v[:, ci * tn:(ci + 1) * tn], onb[0:tn, ci * 128:(ci + 1) * 128], identb[0:tn, 0:tn], onB + [cstb], [p9], ci == 3)
                    TT(ygT_t[:, 4 * hd:4 * hd + 4, cols], p9v[:, 0:4 * tn].rearrange("p (a b) -> p a b", a=4),
                       c[:, K_GGLA:K_GGLA + 4].unsqueeze(2).broadcast_to([128, 4, tn]), ALU.mult, [p9, cst], [ygT[hd]])
                    for kc2 in range(2):
                        kk = 2 * hd + kc2
                        p10 = ps_next()
                        MM(p10.ap[:, :], kpp_tok[0:tn, t, kc2 * 128:(kc2 + 1) * 128], vt[0:tn, :], True, True, [AR[10]] + vtb, [p10], True)
                        STT(Sg[kk].ap[:, :], Sg[kk].ap[:, :], edl_all.ap[:, kk * 4 + t:kk * 4 + t + 1], p10.ap[:, :], ALU.mult, ALU.add,
                            [Sg[kk], edl_all, p10], [Sg[kk]])

                gla_G(0)
                gla_attn(0)
                for t in range(nt):
                    gla_O(t)
                    if t + 1 < nt:
                        gla_G(t + 1)
                        gla_attn(t + 1)
                    gla_tail(t)

        def phase_d(seq, t0, nt, tn):
            TB = nt * tn
            if os.environ.get("KVERB"):
                print("phase_d start nops", S.nops)
            pg = arf(0, 4)
            pgb = arbufs(0, 4)
            DMA("sp", pg[:, :], pg_d[:, :], [], pgb, sem_buf=pgb[0])
            for m in range(16):
                def ld1(s_, m=m):
                    v = s_.ap[:].rearrange("p (k n) -> p k n", k=32)
                    load_w(v[:, 0:16, :], wbs_v[:, 0:16, 128 * m:128 * m + 128], s_)
                    load_w(v[:, 16:32, :], wbs_v[:, 16:32, 128 * m:128 * m + 128], s_)

                def ld3(s_, m=m):
                    load_w(s_.ap[:, 0:2048].rearrange("p (k n) -> p k n", k=16), w_in_v[:, :, C_M + 128 * m:C_M + 128 * m + 128], s_)
                    load_w(s_.ap[:, 2048:4096].rearrange("p (k n) -> p k n", k=16), w_in_v[:, :, C_M + 2048 + 128 * m:C_M + 2048 + 128 * m + 128], s_)
                s3 = W.get(("M", m), ld3)
                s1 = W.get(("BS", m), ld1)
                s2 = W.get(("BG", m), lambda s_, m=m: load_w(s_.ap[:, 0:2048].rearrange("p (k n) -> p k n", k=16),
                                                           wbg_v[:, :, 128 * m:128 * m + 128], s_))
                s1v = s1.ap[:].rearrange("p (k n) -> p k n", k=32)
                s2v = s2.ap[:, 0:2048].rearrange("p (k n) -> p k n", k=16)
                s3a = s3.ap[:, 0:2048].rearrange("p (k n) -> p k n", k=16)
                s3b = s3.ap[:, 2048:4096].rearrange("p (k n) -> p k n", k=16)
                t1, t2 = arf(5 + 2 * (m % 2)), arf(6 + 2 * (m % 2))
                tb = arbufs(5 + 2 * (m % 2), 2)
                pG1 = ps_next()
                for kc in range(16):
                    MM(pG1.ap[:, 0:TB], s3a[:, kc, :], hT_t[:, kc, 0:TB], kc == 0, kc == 15, [s3, hT], [pG1], kc == 15)
                pG2 = ps_next()
                for kc in range(16):
                    MM(pG2.ap[:, 0:TB], s3b[:, kc, :], hT_t[:, kc, 0:TB], kc == 0, kc == 15, [s3, hT], [pG2], kc == 15)
                W.done(s3)
                ACT(t1[:, 0:TB], pG1.ap[:, 0:TB], AF.Tanh, [pG1], tb, scale=0.5)
                ACT(t2[:, 0:TB], pG2.ap[:, 0:TB], AF.Tanh, [pG2], tb, scale=0.5)
                pP1 = ps_next()
                for kc in range(32):
                    MM(pP1.ap[:, 0:TB], s1v[:, kc, :], ysT_t[:, kc, 0:TB], kc == 0, kc == 31, [s1] + ysT, [pP1], kc == 31)
                W.done(s1)
                pP2 = ps_next()
                for kc in range(16):
                    MM(pP2.ap[:, 0:TB], s2v[:, kc, :], ygT_t[:, kc, 0:TB], kc == 0, kc == 15, [s2] + ygT, [pP2], kc == 15)
                W.done(s2)
                STT(t1[:, 0:TB], t1[:, 0:TB], 1.0, pP1.ap[:, 0:TB], ALU.add, ALU.mult, [pP1] + tb, tb)
                STT(t2[:, 0:TB], t2[:, 0:TB], 1.0, pP2.ap[:, 0:TB], ALU.add, ALU.mult, [pP2] + tb, tb)
                TT(mixT_t[:, m, 0:TB], t1[:, 0:TB], t2[:, 0:TB], ALU.add, tb, [mixT])
            if os.environ.get("KVERB"):
                print("phase_d final-proj start nops", S.nops)
            for cc in range(4):
                so = [W.get(("O", cc, i), lambda s_, cc=cc, i=i: load_w(s_.ap[:].rearrange("p (k n) -> p k n", k=16),
                                                                      wo_v[:, :, 512 * cc + 256 * i:512 * cc + 256 * i + 256], s_))
                      for i in range(2)]
                sov = [s_.ap[:].rearrange("p (k n) -> p k n", k=16) for s_ in so]
                for t in range(nt):
                    cols = slice(t * tn, (t + 1) * tn)
                    stage = ysT_t[:, 8 * t:8 * t + 8, :].rearrange("p a b -> p (a b)").bitcast(F32)
                    stb = ysT[2 * t:2 * t + 2]
                    pO = ps_next()
                    for i in range(2):
                        for m in range(16):
                            MM(pO.ap[0:tn, 256 * i:256 * i + 256], mixT_t[:, m, cols], sov[i][:, m, :], m == 0, m == 15, [mixT, so[i]], [pO],
                               m == 15, skip=True)
                    if t == nt - 1:
                        W.done(so[0])
                        W.done(so[1])
                    junk = arb(9)[:, 0:512]
                    CP(stage[0:tn, 512 * cc:512 * cc + 512], pO.ap[0:tn, :], [pO], stb, eng="dve")
                    ACT(junk[0:tn, :], stage[0:tn, 512 * cc:512 * cc + 512], AF.Square, stb, arbufs(9) + [small], scale=0.5,
                        accum=small.ap[0:tn, 8 + 2 * (4 * t + cc):9 + 2 * (4 * t + cc)])
            if os.environ.get("KVERB"):
                print("phase_d epilogue start nops", S.nops)
            for t in range(nt):
                stage = ysT_t[:, 8 * t:8 * t + 8, :].rearrange("p a b -> p (a b)").bitcast(F32)
                stb = ysT[2 * t:2 * t + 2]
                xr = ygT_t[:, 8 * (t % 2):8 * (t % 2) + 8, :].rearrange("p a b -> p (a b)").bitcast(F32)
                xrb = ygT[2 * (t % 2):2 * (t % 2) + 2]
                r0 = t0 + t * tn
                DMA("sp", xr[0:tn, :], x_in[seq][r0:r0 + tn, :], [], xrb, sem_buf=xrb[0])
                TT(small.ap[0:tn, 6:7], small.ap[0:tn, 8 + 8 * t:9 + 8 * t], small.ap[0:tn, 10 + 8 * t:11 + 8 * t], ALU.add, [small], [small])
                TT(small.ap[0:tn, 6:7], small.ap[0:tn, 6:7], small.ap[0:tn, 12 + 8 * t:13 + 8 * t], ALU.add, [small], [small])
                TT(small.ap[0:tn, 6:7], small.ap[0:tn, 6:7], small.ap[0:tn, 14 + 8 * t:15 + 8 * t], ALU.add, [small], [small])
                RSTD(small.ap[0:tn, 7:8], small.ap[0:tn, 6:7], 4.0 / D, 4e-6, [small])
                STT(stage[0:tn, :], stage[0:tn, :], small.ap[0:tn, 7:8], pg[0:tn, :], ALU.mult, ALU.mult, stb + [small] + pgb, stb)
                TT(stage[0:tn, :], stage[0:tn, :], xr[0:tn, :], ALU.add, stb + xrb, stb)
                DMA("sp", y_out[seq][r0:r0 + tn, :], stage[0:tn, :], stb, [], sem_buf=stb[0])

        def init_states(seq):
            if seq == "p":
                MEMSET(hist_t[:], 0.0, hist, eng="pool")
                MEMSET(ST_t[:], 0.0, ST, eng="pool")
                MEMSET(Sg_t[:], 0.0, Sg, eng="pool")
                return
            stg = arf(0, 12)
            sb_ = arbufs(0, 12)
            MEMSET(stg[0:4, :], 0.0, sb_, eng="pool")
            DMA("sp", stg[0:3, :], cs_in[:, :], [], sb_, sem_buf=sb_[0])
            pb = ps_next()
            for cg in range(48):
                TR(pb.ap[:, 4 * cg:4 * cg + 4], stg[0:4, cg * 128:(cg + 1) * 128], IDN[0:4, 0:4], sb_ + [cst], [pb], cg == 47)
            CP(hist_t[:], pb.ap[:, 0:192].rearrange("p (a b) -> p a b", b=4)[:, :, 0:3], [pb], hist)
            stg2 = arf(12, 8).rearrange("p (a b) -> p a b", a=32)
            sb2 = arbufs(12, 8)
            DMA("sp", stg2, ss_in.rearrange("(a p) n -> p a n", p=128), [], sb2, sem_buf=sb2[0])
            for q in range(8):
                pb = ps_next()
                for i in range(4):
                    TR(pb.ap[:, 128 * i:128 * i + 128], stg2[:, 4 * q + i, :], IDN, sb2 + [cst], [pb], i == 3)
                CP(ST[q].ap[:, :], pb.ap[:, :], [pb], [ST[q]])
            DMA("sp", Sg_t[:].rearrange("p (h c) v -> p h c v", h=4), gs_in.rearrange("h (c p) v -> p h c v", p=128), [], Sg, sem_buf=Sg[0])

        def final_states(seq):
            stg = arf(0, 12)
            sb_ = arbufs(0, 12)
            for r in range(3):
                pbs = [ps_next() for _ in range(4)]
                for j in range(16):
                    cg = 16 * r + j
                    pb = pbs[j // 4]
                    TR(pb.ap[0:3, 128 * (j % 4):128 * (j % 4) + 128], hist[cg].ap[:, 0:3], IDN, [hist[cg], cst], [pb], (j % 4) == 3)
                for q in range(4):
                    CP(stg[0:3, 2048 * r + 512 * q:2048 * r + 512 * q + 512], pbs[q].ap[0:3, :], [pbs[q]], sb_)
            DMA("sp", conv_out[seq][:, :], stg[0:3, :], sb_, [], sem_buf=sb_[0])
            stg2 = arf(12, 8).rearrange("p (a b) -> p a b", a=32)
            sb2 = arbufs(12, 8)
            for q in range(8):
                pb = ps_next()
                for i in range(4):
                    TR(pb.ap[:, 128 * i:128 * i + 128], ST[q].ap[:, 128 * i:128 * i + 128], IDN, [ST[q], cst], [pb], i == 3)
                CP(stg2[:, 4 * q:4 * q + 4, :], pb.ap[:, :].rearrange("p (a b) -> p a b", a=4), [pb], sb2)
            DMA("sp", ssd_out[seq].rearrange("(a p) n -> p a n", p=128), stg2, sb2, [], sem_buf=sb2[0])
            DMA("sp", gla_out[seq].rearrange("h (c p) v -> p h c v", p=128), Sg_t[:].rearrange("p (h c) v -> p h c v", h=4), Sg, [], sem_buf=Sg[0])

        def schedule():
            for seq, T in (("s", S_SEQ), ("p", P_SEQ)):
                if seq == "p" and "S" in dbg:
                    continue
                if "i" in dbg:
                    init_states(seq)
                if seq == "s":
                    blocks = [(0, 1, 64)]
                else:
                    nt = min(4, T // 128)
                    blocks = [(b * nt * 128, nt, 128) for b in range(T // (nt * 128))]
                for (t0, nt, tn) in blocks:
                    cur_u[0] = 0
                    if "a" in dbg:
                        phase_a(seq, t0, nt, tn)
                    phase_b(nt, tn)
                    if stages >= 2:
                        phase_c(nt, tn)
                    if stages >= 3:
                        phase_d(seq, t0, nt, tn)
                    cur_blk[0] += 1
                if "f" in dbg:
                    final_states(seq)

        S.dry = True
        W.record = True
        schedule()
        S.dry = False
        W.record = False
        ps_rr[0] = 0
        nblk = cur_blk[0]
        nu = max(e[3] for e in W.plan) + 1 if W.plan else 0
        same = all(W.plan[i][0] == W.plan[i % nu][0] and W.plan[i][3] == i % nu for i in range(len(W.plan))) and len(W.plan) == nu * nblk
        if nblk > 1 and same and not os.environ.get("KNOSCR"):
            wscr[0] = nc.dram_tensor("wscr", [nu, 128, 4096], BF16).ap()
            for sl_ in slots:
                MEMSET(sl_.ap[:, :], 0.0, [sl_], eng="pool")
        cur_blk[0] = 0
        schedule()
        S.wait_all("sp", AR + ysT + ygT + Sg + ST + [mixT])
        S.emit()
    return nc


def make_consts(inp):
    f = np.float32
    cst = np.zeros((128, K_END), f)
    cst[:, K_GPRE:K_GPRE + 16] = inp["norm_pre_gain"][0].reshape(16, 128).T
    cst[:, K_GSSD:K_GSSD + 32] = inp["ssd_norm_gain"][0].reshape(32, 128).T
    cst[:, K_GGLA:K_GGLA + 4] = inp["gla_norm_gain"][0].reshape(4, 128).T
    cw = inp["conv_w"][0]
    cst[:, K_CW:K_CW + 192] = cw.reshape(4, 48, 128).transpose(2, 1, 0).reshape(128, 192)
    cst[:, K_CB:K_CB + 48] = inp["conv_b"][0].reshape(48, 128).T
    cst[:, K_GKB:K_GKB + 8] = inp["gla_gk_b"][0].reshape(8, 128).T
    cst[:, K_DTB:K_DTB + 64] = np.broadcast_to(inp["dt_bias"][0], (128, 64))
    cst[:, K_ALOG:K_ALOG + 64] = np.broadcast_to(inp["a_log"][0], (128, 64))
    cst[:, K_DSK:K_DSK + 64] = np.broadcast_to(inp["d_skip"][0], (128, 64))
    cst[:, K_U:K_U + 128] = np.triu(np.ones((128, 128), f))
    cst[:, K_IDN:K_IDN + 128] = np.eye(128, dtype=f)
    cst[:, K_NEG:K_NEG + 128] = np.tril(np.full((128, 128), -30000.0, f), -1)
    cst[:, K_ONES:K_ONES + 128] = 1.0
    pg = np.ascontiguousarray(np.broadcast_to(inp["norm_post_gain"][0], (128, D))).astype(f)
    return cst, pg


_PROG = {}


def run(inp, P_SEQ, stages=3, n_cores=8, trace=False):
    key = (P_SEQ, stages)
    if key not in _PROG:
        _PROG[key] = build_program(P_SEQ, stages)
    nc = _PROG[key]
    cst, pg = make_consts(inp)
    shared = dict(
        w_in=np.ascontiguousarray(inp["w_in"][0]), wbs=np.ascontiguousarray(inp["w_branch_ssd"][0]),
        wbg=np.ascontiguousarray(inp["w_branch_gla"][0]), wo=np.ascontiguousarray(inp["w_out"][0]),
        gkw=np.ascontiguousarray(inp["gla_gk_w"][0]), cst=cst, pgrep=pg)
    in_maps = []
    for i in range(n_cores):
        m = dict(shared)
        m["xp"] = np.ascontiguousarray(inp["x_prompt"][i, :P_SEQ])
        m["xs"] = np.ascontiguousarray(inp["x_sample"][i])
        m["cs_in"] = np.ascontiguousarray(inp["state_conv_ssd"][0, i])
        m["ss_in"] = np.ascontiguousarray(inp["state_ssd"][0, i]).reshape(4096, 128)
        m["gs_in"] = np.ascontiguousarray(inp["state_gla"][0, i])
        in_maps.append(m)
    res = run_bass_kernel_spmd(nc, in_maps, core_ids=list(range(n_cores)), **({"trace": True} if trace else {}))
    R = res.results
    st = lambda k: np.stack([np.asarray(R[i][k]) for i in range(n_cores)])
    outs = (st("yp"), st("ys"),
            st("conv_p")[None], st("ssd_p").reshape(n_cores, 64, 64, 128)[None], st("gla_p")[None],
            st("conv_s")[None], st("ssd_s").reshape(n_cores, 64, 64, 128)[None], st("gla_s")[None])
    return outs, res


def kernel(**inputs):
    inp = {k: np.asarray(v) for k, v in inputs.items()}
    outs, _ = run(inp, P_SEQ_FULL, 3, 8)
    return tuple(np.ascontiguousarray(o, dtype=np.float32) for o in outs)
```

```python
import numpy as np
from contextlib import ExitStack
import concourse.bass as bass
import concourse.mybir as mybir
from concourse.bass_utils import run_bass_kernel_spmd

F32 = mybir.dt.float32
BF16 = mybir.dt.bfloat16
ALU = mybir.AluOpType
AF = mybir.ActivationFunctionType

D = 2048
NCOL = 20560
P_SEQ_FULL = 4096
S_SEQ = 64
C_Z, C_XS, C_B, C_C, C_DT, C_Q, C_K, C_V, C_G, C_GKL, C_M = 0, 4096, 8192, 9216, 10240, 10304, 11328, 12352, 14400, 16448, 16464
K_GPRE, K_GSSD, K_GGLA, K_CW, K_CB, K_GKB, K_DTB, K_ALOG, K_DSK, K_U, K_IDN, K_NEG, K_ONES, K_END = (
    0, 16, 48, 52, 244, 292, 300, 364, 428, 492, 620, 748, 876, 1004)


class Buf:
    __slots__ = ("ap", "w", "r", "dsem", "dcnt", "name")

    def __init__(self, ap, name=""):
        self.ap = ap
        self.w = None
        self.r = {}
        self.dsem = None
        self.dcnt = 0
        self.name = name


class Eng:
    def __init__(self, name, own_deps):
        self.name = name
        self.ops = []
        self.cnt = 0
        self.sem = None
        self.seen = {}
        self.own_deps = own_deps
        self.pending = False


class Sched:
    def __init__(self, nc, es):
        self.nc = nc
        self.es = es
        self.E = {"pe": Eng("pe", False), "act": Eng("act", True), "dve": Eng("dve", True),
                  "pool": Eng("pool", True), "sp": Eng("sp", False)}
        for k, e in self.E.items():
            e.sem = es.enter_context(nc.semaphore("sem_" + k))
        self.nsem = 0
        import os
        self.dry = False
        self.nops = 0
        self.maxops = int(os.environ.get("KMAXOPS", str(10 ** 9)))

    def new_dsem(self):
        self.nsem += 1
        return self.es.enter_context(self.nc.semaphore("dsem%d" % self.nsem))

    def _deps(self, e, reads, writes):
        deps = {}

        def add(tok):
            if tok is None:
                return
            s, v = tok
            k = id(s)
            if k not in deps or deps[k][1] < v:
                deps[k] = (s, v)

        for b in reads:
            add(b.w)
        for b in writes:
            add(b.w)
            for tok in b.r.values():
                add(tok)
        waits = []
        for k, (s, v) in deps.items():
            if s is e.sem and not e.own_deps:
                continue
            if e.seen.get(k, 0) >= v:
                continue
            e.seen[k] = v
            waits.append((s, v))
        return waits

    def op(self, eng, fn, reads=(), writes=(), inc=True):
        if self.dry:
            return None
        self.nops += 1
        if self.nops > self.maxops:
            return None
        if self.maxops < 10 ** 9:
            inc = True
        e = self.E[eng]
        waits = self._deps(e, reads, writes)
        if inc:
            e.cnt += 1
            tok = (e.sem, e.cnt)
            e.pending = False
        else:
            assert not e.own_deps
            tok = (e.sem, e.cnt + 1)
            e.pending = True
        e.ops.append((waits, fn, (e.sem, 1) if inc else None))
        for b in writes:
            b.w = tok
            b.r = {}
        for b in reads:
            b.r[id(e.sem)] = tok
        return tok

    def dma(self, eng, fn, reads=(), writes=(), sem_buf=None):
        if self.dry:
            return None
        self.nops += 1
        if self.nops > self.maxops:
            return None
        e = self.E[eng]
        waits = self._deps(e, reads, writes)
        sb = sem_buf if sem_buf is not None else (writes[0] if writes else reads[0])
        if sb.dsem is None:
            sb.dsem = self.new_dsem()
        sb.dcnt += 16
        tok = (sb.dsem, sb.dcnt)
        e.ops.append((waits, fn, (sb.dsem, 16)))
        for b in writes:
            b.w = tok
            b.r = {}
        for b in reads:
            b.r[id(sb.dsem)] = tok
        return tok

    def wait_all(self, eng, bufs):
        e = self.E[eng]
        waits = self._deps(e, (), bufs)
        e.ops.append((waits, None, None))

    def emit(self):
        nc = self.nc
        hmap = {"pe": "tensor", "act": "scalar", "dve": "vector", "pool": "gpsimd", "sp": "sync"}
        with nc.Block() as block:
            for k, e in self.E.items():
                assert not e.pending, k

                def body(h, e=e):
                    for waits, fn, inc in e.ops:
                        for s, v in waits:
                            h.wait_ge(s, v)
                        if fn is not None:
                            ins = fn(h)
                            if inc is not None:
                                ins.then_inc(inc[0], inc[1])

                getattr(block, hmap[k])(body)


def build_program(P_SEQ, stages=3, dbg=None):
    import os
    dbg = os.environ.get("KDBG", "ia012f") if dbg is None else dbg
    nc = bass.Bass("TRN2", target_bir_lowering=False)

    def din(name, shape):
        return nc.dram_tensor(name, shape, F32, kind="ExternalInput").ap()

    def dout(name, shape):
        return nc.dram_tensor(name, shape, F32, kind="ExternalOutput").ap()

    x_in = {"p": din("xp", [P_SEQ, D]), "s": din("xs", [S_SEQ, D])}
    cs_in = din("cs_in", [3, 6144])
    ss_in = din("ss_in", [4096, 128])
    gs_in = din("gs_in", [4, 256, 512])
    w_in = din("w_in", [D, NCOL])
    wbs = din("wbs", [4096, D])
    wbg = din("wbg", [D, D])
    wo = din("wo", [D, D])
    gkw = din("gkw", [16, 1024])
    cst_d = din("cst", [128, K_END])
    pg_d = din("pgrep", [128, D])
    y_out = {"p": dout("yp", [P_SEQ, D]), "s": dout("ys", [S_SEQ, D])}
    conv_out = {"p": dout("conv_p", [3, 6144]), "s": dout("conv_s", [3, 6144])}
    ssd_out = {"p": dout("ssd_p", [4096, 128]), "s": dout("ssd_s", [4096, 128])}
    gla_out = {"p": dout("gla_p", [4, 256, 512]), "s": dout("gla_s", [4, 256, 512])}

    w_in_v = w_in.rearrange("(kc p) n -> p kc n", p=128)
    wbs_v = wbs.rearrange("(kc p) n -> p kc n", p=128)
    wbg_v = wbg.rearrange("(kc p) n -> p kc n", p=128)
    wo_v = wo.rearrange("(kc p) n -> p kc n", p=128)

    with ExitStack() as es:
        S = Sched(nc, es)

        def sbt(name, shape, dt):
            return es.enter_context(nc.sbuf_tensor("s_" + name, shape, dt))

        cst = Buf(sbt("cst", [128, K_END], F32), "cst")
        cstb_t = sbt("cstb", [128, 256], BF16)
        cstb = Buf(cstb_t, "cstb")
        identb = cstb_t[:, 0:128]
        negb = cstb_t[:, 128:256]
        wdt = Buf(sbt("wdt", [128, 16, 64], BF16), "wdt")
        wgkl = Buf(sbt("wgkl", [128, 16, 16], BF16), "wgkl")
        Wg = Buf(sbt("Wg", [16, 1024], F32), "Wg")
        hT_t = sbt("hT", [128, 16, 512], BF16)
        hT = Buf(hT_t, "hT")
        ysT_t = sbt("ysT", [128, 32, 512], BF16)
        ysT = [Buf(ysT_t[:, 4 * g:4 * g + 4, :], "ysT%d" % g) for g in range(8)]
        ygT_t = sbt("ygT", [128, 16, 512], BF16)
        ygT = [Buf(ygT_t[:, 4 * h:4 * h + 4, :], "ygT%d" % h) for h in range(4)]
        mixT_t = sbt("mixT", [128, 16, 512], BF16)
        mixT = Buf(mixT_t, "mixT")
        ST_t = sbt("ST", [128, 4096], F32)
        ST = [Buf(ST_t[:, 512 * g:512 * g + 512], "ST%d" % g) for g in range(8)]
        Sg_t = sbt("Sg", [128, 8, 512], F32)
        Sg = [Buf(Sg_t[:, k, :], "Sg%d" % k) for k in range(8)]
        NSLOT = 4
        slots = [Buf(sbt("wslot%d" % i, [128, 4096], BF16), "wslot%d" % i) for i in range(NSLOT)]
        hist_t = sbt("hist", [128, 48, 3], F32)
        hist = [Buf(hist_t[:, c, :], "hist%d" % c) for c in range(48)]
        da_all = Buf(sbt("da_all", [128, 256], F32), "da_all")
        lb_all = Buf(sbt("lb_all", [128, 256], F32), "lb_all")
        esh_all = Buf(sbt("esh_all", [128, 256], F32), "esh_all")
        dec_all = Buf(sbt("dec_all", [128, 256], F32), "dec_all")
        w_all = Buf(sbt("w_all", [128, 256], F32), "w_all")
        dt_all = Buf(sbt("dt_all", [128, 256], F32), "dt_all")
        edl_all = Buf(sbt("edl_all", [128, 32], F32), "edl_all")
        gklT = Buf(sbt("gklT", [16, 512], F32), "gklT")
        small = Buf(sbt("small", [128, 64], F32), "small")
        ssm = Buf(sbt("ssm", [128, 64], F32), "ssm")
        NAR = 20
        ar_t = sbt("arena", [128, NAR * 512], F32)
        AR = [Buf(ar_t[:, 512 * i:512 * i + 512], "ar%d" % i) for i in range(NAR)]
        PS = [Buf(es.enter_context(nc.psum_tensor("ps%d" % i, [128, 512], F32))[:, :], "ps%d" % i) for i in range(8)]
        ps_rr = [0]
        ps_mod = [6]

        def ps_next():
            b = PS[ps_rr[0] % ps_mod[0]]
            ps_rr[0] += 1
            assert S.dry or b.w is None or b.r, "PSUM bank %s re-allocated before its consumers were emitted" % b.name
            return b

        class WStream:
            def __init__(self):
                self.plan = []
                self.idx = 0
                self.emitted = 0
                self.record = True
                self.slot_of = {}
                self.free = list(range(NSLOT))

            def _pump(self):
                while self.emitted < len(self.plan) and self.free:
                    sl = self.free.pop(0)
                    key, loader, blk, u = self.plan[self.emitted]
                    slot = slots[sl]
                    if blk == 0 or wscr[0] is None:
                        loader(slot)
                        if wscr[0] is not None:
                            S.dma("sp", lambda h, u=u, slot=slot: h.dma_start(out=wscr[0][u], in_=slot.ap[:, :]),
                                  reads=[slot], writes=[wbB[sl]], sem_buf=wbB[sl])
                    else:
                        S.dma("sp", lambda h, u=u, slot=slot: h.dma_start(out=slot.ap[:, :], in_=wscr[0][u]),
                              reads=wbB, writes=[slot], sem_buf=rbB[sl])
                    self.slot_of[self.emitted] = sl
                    self.emitted += 1

            def get(self, key, loader):
                if self.record:
                    self.plan.append((key, loader, cur_blk[0], cur_u[0]))
                    cur_u[0] += 1
                    return slots[0]
                assert self.plan[self.idx][0] == key, (self.plan[self.idx][0], key)
                if self.idx >= self.emitted:
                    self._pump()
                assert self.idx < self.emitted, ("no free weight slot", key)
                sl = self.slot_of.pop(self.idx)
                self.idx += 1
                return slots[sl]

            def done(self, slot):
                if self.record:
                    return
                self.free.append(slots.index(slot))
                self._pump()

        W = WStream()
        wscr = [None]
        cur_blk = [0]
        cur_u = [0]
        wbB = [Buf(None, "wb%d" % i) for i in range(NSLOT)]
        rbB = [Buf(None, "rb%d" % i) for i in range(NSLOT)]

        def arf(i, n=1):
            return ar_t[:, 512 * i:512 * (i + n)]

        def arb(i, n=1):
            return ar_t[:, 512 * i:512 * (i + n)].bitcast(BF16)

        def arbufs(i, n=1):
            return AR[i:i + n]

        def MM(out, lhsT, rhs, start, stop, reads, writes, inc, skip=False):
            S.op("pe", lambda h: h.matmul(out, lhsT=lhsT, rhs=rhs, start=start, stop=stop, skip_group_check=skip),
                 reads, writes, inc)

        def TR(out, in_, ident, reads, writes, inc):
            S.op("pe", lambda h: h.transpose(out=out, in_=in_, identity=ident), reads, writes, inc)

        def ACT(out, in_, func, reads, writes, bias=None, scale=None, accum=None):
            kw = {}
            if bias is not None:
                kw["bias"] = bias
            if scale is not None:
                kw["scale"] = scale
            if accum is not None:
                kw["accum_out"] = accum
            S.op("act", lambda h: h.activation(out=out, in_=in_, func=func, **kw), reads, writes)

        def TS(out, in0, s1, s2, op0, op1, reads, writes, eng="dve"):
            if op1 is None:
                S.op(eng, lambda h: h.tensor_scalar(out=out, in0=in0, scalar1=s1, scalar2=None, op0=op0), reads, writes)
            else:
                S.op(eng, lambda h: h.tensor_scalar(out=out, in0=in0, scalar1=s1, scalar2=s2, op0=op0, op1=op1), reads, writes)

        def TT(out, in0, in1, op, reads, writes, eng="dve"):
            S.op(eng, lambda h: h.tensor_tensor(out=out, in0=in0, in1=in1, op=op), reads, writes)

        def STT(out, in0, scalar, in1, op0, op1, reads, writes):
            S.op("dve", lambda h: h.scalar_tensor_tensor(out=out, in0=in0, scalar=scalar, in1=in1, op0=op0, op1=op1), reads, writes)

        def CP(out, in_, reads, writes, eng="dve"):
            if eng == "act":
                S.op("act", lambda h: h.activation(out=out, in_=in_, func=AF.Copy), reads, writes)
            else:
                S.op(eng, lambda h: h.tensor_copy(out=out, in_=in_), reads, writes)

        def MEMSET(out, val, writes, eng="dve"):
            S.op(eng, lambda h: h.memset(out, val), (), writes)

        def DMA(q, out, in_, reads, writes, sem_buf=None):
            S.dma(q, lambda h: h.dma_start(out=out, in_=in_), reads, writes, sem_buf)

        def RSTD(out_col, ss_col, mult, eps, reads_writes, eng="dve"):
            n = ss_col.shape[0]
            TS(out_col, ss_col, mult, eps, ALU.mult, ALU.add, reads_writes, reads_writes, eng=eng)
            mh = ssm if (len(reads_writes) == 1 and reads_writes[0] is ssm) else small
            TT(out_col, out_col, mh.ap[0:n, 63:64], ALU.pow, reads_writes + ([small] if mh is small else []), reads_writes, eng="pool")

        def SOFTPLUS(z, xa, y, tmp, bufs):
            ACT(y, xa, AF.Exp, bufs, bufs)
            TS(y, y, 1.0, None, ALU.add, None, bufs, bufs)
            TS(z, xa, 0.0, 0.35, ALU.max, ALU.add, bufs, bufs)
            for _ in range(4):
                ACT(tmp, z, AF.Exp, bufs, bufs, scale=-1.0)
                TT(tmp, tmp, y, ALU.mult, bufs, bufs)
                STT(z, z, -1.0, tmp, ALU.add, ALU.add, bufs, bufs)

        c = cst.ap
        U = c[:, K_U:K_U + 128]
        IDN = c[:, K_IDN:K_IDN + 128]
        ONES = c[:, K_ONES:K_ONES + 128]

        DMA("sp", cst.ap[:], cst_d[:, :], [], [cst])
        DMA("sp", Wg.ap[:], gkw[:, :], [], [Wg])
        DMA("pool", wdt.ap[:], w_in_v[:, :, C_DT:C_DT + 64], [], [wdt])
        DMA("pool", wgkl.ap[:], w_in_v[:, :, C_GKL:C_GKL + 16], [], [wgkl])
        MEMSET(small.ap[:, 63:64], -0.5, [small])
        MEMSET(ssm.ap[:, 63:64], -0.5, [ssm])
        TS(c[:, K_CW:K_GKB], c[:, K_CW:K_GKB], 0.5, None, ALU.mult, None, [cst], [cst])
        TS(c[:, K_GKB:K_DTB], c[:, K_GKB:K_DTB], -1.0, None, ALU.mult, None, [cst], [cst])
        ACT(c[:, K_ALOG:K_DSK], c[:, K_ALOG:K_DSK], AF.Exp, [cst], [cst])
        TS(c[:, K_ALOG:K_DSK], c[:, K_ALOG:K_DSK], -1.0, None, ALU.mult, None, [cst], [cst])
        CP(cstb_t[:, 0:128], c[:, K_IDN:K_IDN + 128], [cst], [cstb])
        CP(cstb_t[:, 128:256], c[:, K_NEG:K_NEG + 128], [cst], [cstb])

        def load_w(dst_ap, src_ap, slot):
            DMA("pool", dst_ap, src_ap, [], [slot], sem_buf=slot)

        def phase_a(seq, t0, nt, tn):
            for t in range(nt):
                xt = ysT_t[:, 8 * t:8 * t + 8, :].rearrange("p a b -> p (a b)").bitcast(F32)
                xtb = ysT[2 * t:2 * t + 2]
                xn = ygT_t[:, 4 * t:4 * t + 4, :].rearrange("p a b -> p (a b)")
                junk = mixT_t[:, 0:4, :].rearrange("p a b -> p (a b)")
                r0 = t0 + t * tn
                DMA("sp", xt[0:tn, :], x_in[seq][r0:r0 + tn, :], [], xtb, sem_buf=xtb[0])
                ACT(junk[0:tn, :], xt[0:tn, :], AF.Square, xtb, [mixT, small], accum=small.ap[0:tn, 0:1])
                RSTD(small.ap[0:tn, 1:2], small.ap[0:tn, 0:1], 1.0 / D, 1e-6, [small])
                TS(xn[0:tn, :], xt[0:tn, :], small.ap[0:tn, 1:2], None, ALU.mult, None, xtb + [small], [ygT[t]])
                for half in range(2):
                    pb = ps_next()
                    pv = pb.ap[:].bitcast(BF16)
                    for k8 in range(8):
                        kc = half * 8 + k8
                        TR(pv[:, k8 * tn:(k8 + 1) * tn], xn[0:tn, kc * 128:(kc + 1) * 128], identb[0:tn, 0:tn],
                           [ygT[t], cstb], [pb], inc=(k8 == 7))
                    TT(hT_t[:, half * 8:half * 8 + 8, t * tn:(t + 1) * tn],
                       pv[:, 0:8 * tn].rearrange("p (a b) -> p a b", a=8),
                       c[:, K_GPRE + half * 8:K_GPRE + half * 8 + 8].unsqueeze(2).broadcast_to([128, 8, tn]),
                       ALU.mult, [pb, cst], [hT])

        def phase_b0(nt, tn):
            Wd = nt * 64
            a0 = arf(0)
            a1 = arf(11)
            xa, yy = a0[0:tn, 0:Wd], a0[0:tn, 256:256 + Wd]
            tmp = a1[0:tn, 0:Wd]
            scr = a1[:, 256:256 + Wd]
            bb = [AR[0], AR[11]]
            v3 = lambda ap_: ap_.rearrange("p (a b) -> p a b", b=64)
            pb = ps_next()
            for t in range(nt):
                cols = slice(t * tn, (t + 1) * tn)
                for kc in range(16):
                    MM(pb.ap[0:tn, t * 64:(t + 1) * 64], hT_t[:, kc, cols], wdt.ap[:, kc, :], kc == 0, kc == 15, [hT, wdt], [pb], kc == 15, skip=True)
            TT(v3(xa), v3(pb.ap[0:tn, 0:Wd]), c[0:tn, K_DTB:K_DTB + 64].unsqueeze(1).broadcast_to([tn, nt, 64]), ALU.add, [pb, cst], bb)
            SOFTPLUS(dt_all.ap[0:tn, 0:Wd], xa, yy, tmp, bb + [dt_all])
            TT(v3(da_all.ap[0:tn, 0:Wd]), v3(dt_all.ap[0:tn, 0:Wd]), c[0:tn, K_ALOG:K_ALOG + 64].unsqueeze(1).broadcast_to([tn, nt, 64]), ALU.mult,
               [dt_all, cst], [da_all])
            pb2 = ps_next()
            for t in range(nt):
                MM(pb2.ap[0:tn, t * 64:(t + 1) * 64], U[0:tn, 0:tn], da_all.ap[0:tn, t * 64:(t + 1) * 64], True, True, [cst, da_all], [pb2], True, skip=True)
                MM(pb2.ap[:, 256 + t * 64:256 + (t + 1) * 64], ONES[0:tn, 0:128], da_all.ap[0:tn, t * 64:(t + 1) * 64], True, True, [cst, da_all], [pb2], True, skip=True)
            TS(lb_all.ap[0:tn, 0:Wd], pb2.ap[0:tn, 0:Wd], -1.0, None, ALU.mult, None, [pb2], [lb_all])
            ACT(esh_all.ap[0:tn, 0:Wd], lb_all.ap[0:tn, 0:Wd], AF.Exp, [lb_all], [esh_all], scale=-1.0)
            TS(esh_all.ap[0:tn, 0:Wd], esh_all.ap[0:tn, 0:Wd], 0.5, None, ALU.mult, None, [esh_all], [esh_all])
            CP(scr, pb2.ap[:, 256:256 + Wd], [pb2], bb)
            ACT(dec_all.ap[:, 0:Wd], scr, AF.Exp, bb, [dec_all])
            TT(tmp, lb_all.ap[0:tn, 0:Wd], pb2.ap[0:tn, 256:256 + Wd], ALU.add, [lb_all, pb2], bb)
            ACT(tmp, tmp, AF.Exp, bb, bb)
            TT(w_all.ap[0:tn, 0:Wd], tmp, dt_all.ap[0:tn, 0:Wd], ALU.mult, bb + [dt_all], [w_all])

        st1 = {}

        def ssd_stage1_chunk(g, ci, nt, tn, par):
            TB = nt * tn
            base = 1 + par * 5
            xsT = arb(base, 2).rearrange("p (a b) -> p a b", a=4)
            BT = arf(base + 2)
            CT = arf(base + 3)
            BTb = arb(base + 4)[:, 0:512]
            gb = arbufs(base, 5)
            if ci == 0:
                st1["A"] = W.get(("xsA", g), lambda s_, g=g: load_w(s_.ap[:].rearrange("p (k n) -> p k n", k=16),
                                                                   w_in_v[:, :, C_XS + 512 * g:C_XS + 512 * g + 256], s_))
            if ci == 2:
                st1["B"] = W.get(("xsB", g), lambda s_, g=g: load_w(s_.ap[:].rearrange("p (k n) -> p k n", k=16),
                                                                   w_in_v[:, :, C_XS + 512 * g + 256:C_XS + 512 * g + 512], s_))
            if ci == 4:
                def ldc(s_, g=g):
                    v = s_.ap[:].rearrange("p (k n) -> p k n", k=16)
                    load_w(v[:, :, 0:128], w_in_v[:, :, C_B + 128 * g:C_B + 128 * g + 128], s_)
                    load_w(v[:, :, 128:256], w_in_v[:, :, C_C + 128 * g:C_C + 128 * g + 128], s_)
                st1["C"] = W.get(("BC", g), ldc)
            if ci < 2:
                sl, co, cg = st1["A"], ci * 128, 4 * g + ci
            elif ci < 4:
                sl, co, cg = st1["B"], (ci - 2) * 128, 4 * g + ci
            elif ci == 4:
                sl, co, cg = st1["C"], 0, 32 + g
            else:
                sl, co, cg = st1["C"], 128, 40 + g
            sv = sl.ap[:].rearrange("p (k n) -> p k n", k=16)
            if True:
                pb = ps_next()
                for kc in range(16):
                    MM(pb.ap[:, 0:TB], sv[:, kc, co:co + 128], hT_t[:, kc, 0:TB], kc == 0, kc == 15, [sl, hT], [pb], kc == 15)
                if ci in (1, 3, 5):
                    W.done(sl)
                acc_u, th_u = 13, 14
                acc = arf(acc_u)
                th = arf(th_u)
                ab = [AR[acc_u], AR[th_u]]
                w = lambda i: c[:, K_CW + cg * 4 + i:K_CW + cg * 4 + i + 1]
                X = pb.ap
                ACT(acc[:, 0:TB], X[:, 0:TB], AF.Identity, [pb, cst], ab, bias=c[:, K_CB + cg:K_CB + cg + 1], scale=w(3))
                for sh, wi in ((1, 2), (2, 1), (3, 0)):
                    STT(acc[:, sh:TB], X[:, 0:TB - sh], w(wi), acc[:, sh:TB], ALU.mult, ALU.add, [pb, cst] + ab, ab)
                hc = hist[cg]
                STT(acc[:, 0:3], hc.ap[:, 0:3], w(0), acc[:, 0:3], ALU.mult, ALU.add, [hc, cst] + ab, ab)
                STT(acc[:, 0:2], hc.ap[:, 1:3], w(1), acc[:, 0:2], ALU.mult, ALU.add, [hc, cst] + ab, ab)
                STT(acc[:, 0:1], hc.ap[:, 2:3], w(2), acc[:, 0:1], ALU.mult, ALU.add, [hc, cst] + ab, ab)
                CP(hist_t[:, cg, :], X[:, TB - 3:TB], [pb] + ab, [hc])
                ACT(th[:, 0:TB], acc[:, 0:TB], AF.Tanh, ab, ab)
                if ci < 4:
                    dst = xsT[:, ci, 0:TB]
                elif ci == 4:
                    dst = BT[:, 0:TB]
                else:
                    dst = CT[:, 0:TB]
                STT(dst, th[:, 0:TB], 1.0, acc[:, 0:TB], ALU.add, ALU.mult, ab, gb)
                if ci == 4:
                    CP(BTb[:, 0:TB], BT[:, 0:TB], gb, gb, eng="act")

        def ssd_stage2(g, nt, tn, par, hook=None, carry=None):
            base = 1 + par * 5
            xsT = arb(base, 2).rearrange("p (a b) -> p a b", a=4)
            BT = arf(base + 2)
            CT = arf(base + 3)
            BTb = arb(base + 4)[:, 0:512]
            gb = arbufs(base, 5)
            xs_tok = arb(15)[:, 0:512]
            xsd = arb(15)[:, 512:1024]
            xw = arb(16)[:, 0:512]
            Btok = arb(16)[:, 512:640]
            cbm = arb(16)[:, 640:768]
            tb1 = arbufs(15, 2)
            LT = arb(17)[:, :].rearrange("p (a b) -> p a b", a=8)
            MT = arb(18)[:, :].rearrange("p (a b) -> p a b", a=8)
            tb2 = arbufs(17, 2)
            y1sb = arf(19)
            tb3 = arbufs(19)
            yv = arf(0)
            tz = arf(11)
            junkf = arf(19)
            tb4 = arbufs(0)
            tb5 = arbufs(11)
            tb6 = arbufs(12)
            pending = [carry]
            chain_pending = [None]
            ps_mod[0] = 4
            sZ = [W.get(("Z", g, i), lambda s_, g=g, i=i: load_w(s_.ap[:].rearrange("p (k n) -> p k n", k=16),
                                                               w_in_v[:, :, C_Z + 512 * g + 256 * i:C_Z + 512 * g + 256 * i + 256], s_))
                  for i in range(2)]
            sZv = [s_.ap[:].rearrange("p (k n) -> p k n", k=16) for s_ in sZ]
            for t in range(nt):
                cols = slice(t * tn, (t + 1) * tn)
                p1 = ps_next()
                p1v = p1.ap[:].bitcast(BF16)
                for ci in range(4):
                    TR(p1v[0:tn, ci * 128:(ci + 1) * 128], xsT[:, ci, cols], identb, gb + [cstb], [p1], ci == 3)
                TR(p1v[0:tn, 512:640], BTb[:, cols], identb, gb + [cstb], [p1], True)
                CP(Btok[0:tn, :], p1v[0:tn, 512:640], [p1], tb1, eng="act")
                p13 = p1v[0:tn, 0:512].rearrange("p (a b) -> p a b", a=8)
                TT(xs_tok[0:tn, :].rearrange("p (a b) -> p a b", a=8), p13,
                   dt_all.ap[0:tn, t * 64 + 8 * g:t * 64 + 8 * g + 8].unsqueeze(2).broadcast_to([tn, 8, 64]), ALU.mult, [p1, dt_all], tb1)
                TT(xsd[0:tn, :].rearrange("p (a b) -> p a b", a=8), p13,
                   c[0:tn, K_DSK + 8 * g:K_DSK + 8 * g + 8].unsqueeze(2).broadcast_to([tn, 8, 64]), ALU.mult, [p1, cst], tb1)
                TT(xw[0:tn, :].rearrange("p (a b) -> p a b", a=8), p13,
                   w_all.ap[0:tn, t * 64 + 8 * g:t * 64 + 8 * g + 8].unsqueeze(2).broadcast_to([tn, 8, 64]), ALU.mult, [p1, w_all], tb1)
                p2 = ps_next()
                MM(p2.ap[0:tn, 0:tn], BT[:, cols], CT[:, cols], True, True, gb, [p2], True)
                TT(cbm[0:tn, 0:tn], p2.ap[0:tn, 0:tn], U[0:tn, 0:tn], ALU.mult, [p2, cst], tb1)
                for half in range(2):
                    p3 = ps_next()
                    for h4 in range(4):
                        hh = half * 4 + h4
                        h = 8 * g + hh
                        o = p3.ap[0:tn, h4 * 128:h4 * 128 + tn]
                        MM(o, da_all.ap[0:tn, t * 64 + h:t * 64 + h + 1].broadcast_to([tn, tn]), U[0:tn, 0:tn], True, False, [da_all, cst], [p3], False, skip=True)
                        MM(o, identb[0:tn, 0:tn], negb[0:tn, 0:tn], False, True, [cstb], [p3], h4 == 3, skip=True)
                    for h4 in range(4):
                        hh = half * 4 + h4
                        h = 8 * g + hh
                        ACT(arb(17)[0:tn, hh * 128:hh * 128 + tn], p3.ap[0:tn, h4 * 128:h4 * 128 + tn], AF.Exp, [p3, lb_all], tb2,
                            bias=lb_all.ap[0:tn, t * 64 + h:t * 64 + h + 1], scale=1.0)
                TT(MT[0:tn, :, 0:tn], LT[0:tn, :, 0:tn], cbm[0:tn, 0:tn].unsqueeze(1).broadcast_to([tn, 8, tn]), ALU.mult, tb2 + tb1, tb2)
                if chain_pending[0] is not None:
                    chain_pending[0]()
                    chain_pending[0] = None
                p8 = PS[6 + t % 2]
                MM(p8.ap[:, :], Btok[0:tn, :], xw[0:tn, :], True, True, tb1, [p8], True)
                p6 = ps_next()
                for i in range(2):
                    for kc in range(16):
                        MM(p6.ap[0:tn, 256 * i:256 * i + 256], hT_t[:, kc, cols], sZv[i][:, kc, :], kc == 0, kc == 15, [hT, sZ[i]], [p6],
                           kc == 15, skip=True)
                if t == nt - 1:
                    W.done(sZ[0])
                    W.done(sZ[1])
                ACT(tz[0:tn, :], p6.ap[0:tn, :], AF.Tanh, [p6], tb5, scale=0.5)
                STT(tz[0:tn, :], tz[0:tn, :], 1.0, p6.ap[0:tn, :], ALU.add, ALU.mult, [p6] + tb5, tb5)
                if pending[0] is not None:
                    if hook is not None and t > 0:
                        hook(t - 1)
                    pending[0]()
                    pending[0] = None
                p4 = PS[4]
                MM(p4.ap[0:tn, :], identb[0:tn, 0:tn], xsd[0:tn, :], True, False, [cstb] + tb1, [p4], False, skip=True)
                for hh in range(8):
                    MM(p4.ap[0:tn, hh * 64:(hh + 1) * 64], MT[0:tn, hh, 0:tn], xs_tok[0:tn, hh * 64:(hh + 1) * 64], False, hh == 7,
                       tb2 + tb1, [p4], hh == 7, skip=True)
                p5 = PS[5]
                MM(p5.ap[0:tn, :], CT[:, cols], ST[g].ap[:, :], True, True, gb + [ST[g]], [p5], True)

                def chain(t=t, cols=cols, p4=p4, p5=p5, p8=p8):
                    TT(y1sb[0:tn, :].rearrange("p (a b) -> p a b", a=8), p5.ap[0:tn, :].rearrange("p (a b) -> p a b", a=8),
                       esh_all.ap[0:tn, t * 64 + 8 * g:t * 64 + 8 * g + 8].unsqueeze(2).broadcast_to([tn, 8, 64]), ALU.mult, [p5, esh_all], tb3)
                    STT(yv[0:tn, :], p4.ap[0:tn, :], 0.5, y1sb[0:tn, :], ALU.mult, ALU.add, [p4] + tb3, tb4)
                    st3 = ST[g].ap[:, :].rearrange("p (a b) -> p a b", a=8)
                    TT(st3, st3, dec_all.ap[:, t * 64 + 8 * g:t * 64 + 8 * g + 8].unsqueeze(2).broadcast_to([128, 8, 64]), ALU.mult,
                       [ST[g], dec_all], [ST[g]], eng="pool")
                    TT(ST[g].ap[:, :], ST[g].ap[:, :], p8.ap[:, :], ALU.add, [ST[g], p8], [ST[g]])
                    TT(yv[0:tn, :], yv[0:tn, :], tz[0:tn, :], ALU.mult, tb4 + tb5, tb4, eng="pool")
                    ACT(junkf[0:tn, :], yv[0:tn, :], AF.Square, tb4, tb3 + [ssm], accum=ssm.ap[0:tn, 2:3])
                    RSTD(ssm.ap[0:tn, 3:4], ssm.ap[0:tn, 2:3], 1.0 / 512, 1e-5, [ssm], eng="pool")
                    yn = arb(12)[:, (t % 2) * 512:(t % 2) * 512 + 512]
                    TS(yn[0:tn, :], yv[0:tn, :], ssm.ap[0:tn, 3:4], 0.0, ALU.mult, ALU.add, tb4 + [ssm], tb6, eng="pool")

                    def ytrans():
                        p7 = ps_next()
                        p7v = p7.ap[:].bitcast(BF16)
                        for ci in range(4):
                            TR(p7v[:, ci * tn:(ci + 1) * tn], yn[0:tn, ci * 128:(ci + 1) * 128], identb[0:tn, 0:tn], tb6 + [cstb], [p7], ci == 3)
                        TT(ysT_t[:, 4 * g:4 * g + 4, cols], p7v[:, 0:4 * tn].rearrange("p (a b) -> p a b", a=4),
                           c[:, K_GSSD + 4 * g:K_GSSD + 4 * g + 4].unsqueeze(2).broadcast_to([128, 4, tn]), ALU.mult, [p7, cst], [ysT[g]])
                    pending[0] = ytrans
                chain_pending[0] = chain
            chain_pending[0]()
            if hook is not None:
                hook(nt - 1)
            ps_mod[0] = 6
            return pending[0]

        def phase_b(nt, tn):
            if "0" in dbg:
                phase_b0(nt, tn)
            if "1" not in dbg:
                return
            for ci in range(6):
                ssd_stage1_chunk(0, ci, nt, tn, 0)
            carry = None
            for g in range(8):
                def hook(t, g=g):
                    if g + 1 < 8:
                        for ci in range(6):
                            if min(ci // 2, nt - 1) == t:
                                ssd_stage1_chunk(g + 1, ci, nt, tn, (g + 1) % 2)
                if "2" in dbg:
                    carry = ssd_stage2(g, nt, tn, g % 2, hook, carry)
                else:
                    for t in range(nt):
                        hook(t)
            if carry is not None:
                ps_mod[0] = 4
                carry()
                ps_mod[0] = 6

        def phase_c(nt, tn):
            TB = nt * tn
            pb = ps_next()
            for kc in range(16):
                MM(pb.ap[0:16, 0:TB], wgkl.ap[:, kc, :], hT_t[:, kc, 0:TB], kc == 0, kc == 15, [wgkl, hT], [pb], kc == 15)
            CP(gklT.ap[0:16, 0:TB], pb.ap[0:16, 0:TB], [pb], [gklT], eng="act")
            qpT = arf(1, 2).rearrange("p (a b) -> p a b", a=2)
            kpT = arf(3, 2).rearrange("p (a b) -> p a b", a=2)
            qb = arbufs(1, 2)
            kb = arbufs(3, 2)
            ebuf, lbuf, clbuf, eqb = arf(5), arf(6), arf(7), arf(8)
            eb = arbufs(5, 4)
            kpp_pending = [None]
            kpp_tok = arb(10)[:, :].rearrange("p (a b) -> p a b", a=4)
            kb2 = arbufs(9, 2)
            v_tok = arb(15)[:, 0:512]
            onb = arb(15)[:, 512:1024]
            am = arb(16)[:, 0:128]
            junk = arb(16)[:, 512:1024]
            tg = arf(17)
            vb = arbufs(15, 3)
            for hd in range(4):
                sQK = W.get(("Q", hd), lambda s_, hd=hd: load_w(s_.ap[:].rearrange("p (k n) -> p k n", k=16),
                                                               w_in_v[:, :, C_Q + 256 * hd:C_Q + 256 * hd + 256], s_))
                sQKv = sQK.ap[:].rearrange("p (k n) -> p k n", k=16)
                sKK = W.get(("K", hd), lambda s_, hd=hd: load_w(s_.ap[:].rearrange("p (k n) -> p k n", k=16),
                                                               w_in_v[:, :, C_K + 256 * hd:C_K + 256 * hd + 256], s_))
                sKKv = sKK.ap[:].rearrange("p (k n) -> p k n", k=16)
                for kc2 in range(2):
                    kk = 2 * hd + kc2
                    p1 = ps_next()
                    MM(p1.ap[:, 0:TB], Wg.ap[0:16, kk * 128:(kk + 1) * 128], gklT.ap[0:16, 0:TB], True, True, [Wg, gklT], [p1], True)
                    TS(ebuf[:, 0:TB], p1.ap[:, 0:TB], -1.0, c[:, K_GKB + kk:K_GKB + kk + 1], ALU.mult, ALU.add, [p1, cst], eb)
                    ACT(eqb[:, 0:TB], ebuf[:, 0:TB], AF.Exp, eb, eb)
                    ACT(lbuf[:, 0:TB], eqb[:, 0:TB], AF.Ln, eb, eb, bias=1.0)
                    for t in range(nt):
                        cols = slice(t * tn, (t + 1) * tn)
                        S.op("dve", lambda h, cols=cols: h.tensor_tensor_scan(out=clbuf[:, cols], data0=ONES[:, 0:tn], data1=lbuf[:, cols],
                                                                             initial=0.0, op0=ALU.mult, op1=ALU.add), eb + [cst], eb)
                        ACT(edl_all.ap[:, kk * 4 + t:kk * 4 + t + 1], clbuf[:, (t + 1) * tn - 1:(t + 1) * tn], AF.Exp, eb, [edl_all], scale=-1.0 / 16)
                    p2 = ps_next()
                    for kc in range(16):
                        MM(p2.ap[:, 0:TB], sQKv[:, kc, kc2 * 128:(kc2 + 1) * 128], hT_t[:, kc, 0:TB], kc == 0, kc == 15, [sQK, hT], [p2], kc == 15)
                    ACT(eqb[:, 0:TB], clbuf[:, 0:TB], AF.Exp, eb, eb, scale=-1.0 / 16)
                    STT(qpT[:, kc2, 0:TB], p2.ap[:, 0:TB], 1.0 / 16, eqb[:, 0:TB], ALU.mult, ALU.mult, [p2] + eb, qb)
                    p3 = ps_next()
                    for kc in range(16):
                        MM(p3.ap[:, 0:TB], sKKv[:, kc, kc2 * 128:(kc2 + 1) * 128], hT_t[:, kc, 0:TB], kc == 0, kc == 15, [sKK, hT], [p3], kc == 15)
                    if kc2 == 1:
                        W.done(sQK)
                        W.done(sKK)
                        kpp_pending[0]()
                    ACT(eqb[:, 0:TB], clbuf[:, 0:TB], AF.Exp, eb, eb, scale=1.0 / 16)
                    TT(kpT[:, kc2, 0:TB], p3.ap[:, 0:TB], eqb[:, 0:TB], ALU.mult, [p3] + eb, kb)
                    for t in range(nt):
                        cols = slice(t * tn, (t + 1) * tn)
                        TS(arb(9)[:, (kc2 * 4 + t) * 128:(kc2 * 4 + t) * 128 + tn], kpT[:, kc2, cols], edl_all.ap[:, kk * 4 + t:kk * 4 + t + 1],
                           None, ALU.mult, None, kb + [edl_all], [AR[9]])

                    def kpp_tr(kc2=kc2):
                        for t in range(nt):
                            p4 = ps_next()
                            p4v = p4.ap[:].bitcast(BF16)
                            TR(p4v[0:tn, 0:128], arb(9)[:, (kc2 * 4 + t) * 128:(kc2 * 4 + t) * 128 + tn], identb, [AR[9], cstb], [p4], True)
                            CP(arb(10)[0:tn, t * 256 + kc2 * 128:t * 256 + (kc2 + 1) * 128], p4v[0:tn, 0:128], [p4], [AR[10]], eng="act")
                    kpp_pending[0] = kpp_tr
                sV = [W.get(("V", hd, i), lambda s_, hd=hd, i=i: load_w(s_.ap[:].rearrange("p (k n) -> p k n", k=16),
                                                                      w_in_v[:, :, C_V + 512 * hd + 256 * i:C_V + 512 * hd + 256 * i + 256], s_))
                      for i in range(2)]
                sVv = [s_.ap[:].rearrange("p (k n) -> p k n", k=16) for s_ in sV]
                for t in range(nt):
                    cols = slice(t * tn, (t + 1) * tn)
                    p5 = ps_next()
                    for i in range(2):
                        for kc in range(16):
                            MM(p5.ap[0:tn, 256 * i:256 * i + 256], hT_t[:, kc, cols], sVv[i][:, kc, :], kc == 0, kc == 15, [hT, sV[i]], [p5],
                               kc == 15, skip=True)
                    vt = arb(11 + t // 2)[:, (t % 2) * 512:(t % 2) * 512 + 512]
                    CP(vt[0:tn, :], p5.ap[0:tn, :], [p5], arbufs(11 + t // 2), eng="act")
                    if t == nt - 1:
                        W.done(sV[0])
                        W.done(sV[1])
                kpp_pending[0]()
                sG = [W.get(("G", hd, i), lambda s_, hd=hd, i=i: load_w(s_.ap[:].rearrange("p (k n) -> p k n", k=16),
                                                                      w_in_v[:, :, C_G + 512 * hd + 256 * i:C_G + 512 * hd + 256 * i + 256], s_))
                      for i in range(2)]
                sGv = [s_.ap[:].rearrange("p (k n) -> p k n", k=16) for s_ in sG]
                onB, amB, jkB = arbufs(15), arbufs(16), arbufs(19)
                junkf = arf(19)

                def gla_G(t):
                    cols = slice(t * tn, (t + 1) * tn)
                    tgt = arf(17 + t % 2)
                    tgB = arbufs(17 + t % 2)
                    p6 = ps_next()
                    for i in range(2):
                        for kc in range(16):
                            MM(p6.ap[0:tn, 256 * i:256 * i + 256], hT_t[:, kc, cols], sGv[i][:, kc, :], kc == 0, kc == 15, [hT, sG[i]], [p6],
                               kc == 15, skip=True)
                    if t == nt - 1:
                        W.done(sG[0])
                        W.done(sG[1])
                    ACT(tgt[0:tn, :], p6.ap[0:tn, :], AF.Tanh, [p6], tgB, scale=0.5)
                    STT(tgt[0:tn, :], tgt[0:tn, :], 1.0, p6.ap[0:tn, :], ALU.add, ALU.mult, [p6] + tgB, tgB)

                def gla_attn(t):
                    cols = slice(t * tn, (t + 1) * tn)
                    p7 = ps_next()
                    for kc2 in range(2):
                        MM(p7.ap[0:tn, 0:tn], kpT[:, kc2, cols], qpT[:, kc2, cols], kc2 == 0, kc2 == 1, kb + qb, [p7], kc2 == 1)
                    TT(am[0:tn, 0:tn], p7.ap[0:tn, 0:tn], U[0:tn, 0:tn], ALU.mult, [p7, cst], amB)

                def gla_O(t):
                    cols = slice(t * tn, (t + 1) * tn)
                    vt = arb(11 + t // 2)[:, (t % 2) * 512:(t % 2) * 512 + 512]
                    vtb = arbufs(11 + t // 2)
                    tgt = arf(17 + t % 2)
                    tgB = arbufs(17 + t % 2)
                    p8 = ps_next()
                    MM(p8.ap[0:tn, :], am[0:tn, 0:tn], vt[0:tn, :], True, False, amB + vtb, [p8], False, skip=True)
                    for kc2 in range(2):
                        kk = 2 * hd + kc2
                        MM(p8.ap[0:tn, :], qpT[:, kc2, cols], Sg[kk].ap[:, :], False, kc2 == 1, qb + [Sg[kk]], [p8], kc2 == 1, skip=True)
                    ACT(junkf[0:tn, :], p8.ap[0:tn, :], AF.Square, [p8], jkB + [small], accum=small.ap[0:tn, 4:5])
                    RSTD(small.ap[0:tn, 5:6], small.ap[0:tn, 4:5], 4.0 / 512, 4e-6, [small])
                    STT(onb[0:tn, :], p8.ap[0:tn, :], small.ap[0:tn, 5:6], tgt[0:tn, :], ALU.mult, ALU.mult, [p8, small] + tgB, onB)

                def gla_tail(t):
                    cols = slice(t * tn, (t + 1) * tn)
                    vt = arb(11 + t // 2)[:, (t % 2) * 512:(t % 2) * 512 + 512]
                    vtb = arbufs(11 + t // 2)
                    p9 = ps_next()
                    p9v = p9.ap[:].bitcast(BF16)
                    for ci in range(4):
                        TR(p9v[:, ci * tn:(ci + 1) * tn], onb[0:tn, ci * 128:(ci + 1) * 128], identb[0:tn, 0:tn], onB + [cstb], [p9], ci == 3)
                    TT(ygT_t[:, 4 * hd:4 * hd + 4, cols], p9v[:, 0:4 * tn].rearrange("p (a b) -> p a b", a=4),
                       c[:, K_GGLA:K_GGLA + 4].unsqueeze(2).broadcast_to([128, 4, tn]), ALU.mult, [p9, cst], [ygT[hd]])
                    for kc2 in range(2):
                        kk = 2 * hd + kc2
                        p10 = ps_next()
                        MM(p10.ap[:, :], kpp_tok[0:tn, t, kc2 * 128:(kc2 + 1) * 128], vt[0:tn, :], True, True, [AR[10]] + vtb, [p10], True)
                        STT(Sg[kk].ap[:, :], Sg[kk].ap[:, :], edl_all.ap[:, kk * 4 + t:kk * 4 + t + 1], p10.ap[:, :], ALU.mult, ALU.add,
                            [Sg[kk], edl_all, p10], [Sg[kk]])

                gla_G(0)
                gla_attn(0)
                for t in range(nt):
                    gla_O(t)
                    if t + 1 < nt:
                        gla_G(t + 1)
                        gla_attn(t + 1)
                    gla_tail(t)

        def phase_d(seq, t0, nt, tn):
            TB = nt * tn
            if os.environ.get("KVERB"):
                print("phase_d start nops", S.nops)
            pg = arf(0, 4)
            pgb = arbufs(0, 4)
            DMA("sp", pg[:, :], pg_d[:, :], [], pgb, sem_buf=pgb[0])
            for m in range(16):
                def ld1(s_, m=m):
                    v = s_.ap[:].rearrange("p (k n) -> p k n", k=32)
                    load_w(v[:, 0:16, :], wbs_v[:, 0:16, 128 * m:128 * m + 128], s_)
                    load_w(v[:, 16:32, :], wbs_v[:, 16:32, 128 * m:128 * m + 128], s_)

                def ld3(s_, m=m):
                    load_w(s_.ap[:, 0:2048].rearrange("p (k n) -> p k n", k=16), w_in_v[:, :, C_M + 128 * m:C_M + 128 * m + 128], s_)
                    load_w(s_.ap[:, 2048:4096].rearrange("p (k n) -> p k n", k=16), w_in_v[:, :, C_M + 2048 + 128 * m:C_M + 2048 + 128 * m + 128], s_)
                s3 = W.get(("M", m), ld3)
                s1 = W.get(("BS", m), ld1)
                s2 = W.get(("BG", m), lambda s_, m=m: load_w(s_.ap[:, 0:2048].rearrange("p (k n) -> p k n", k=16),
                                                           wbg_v[:, :, 128 * m:128 * m + 128], s_))
                s1v = s1.ap[:].rearrange("p (k n) -> p k n", k=32)
                s2v = s2.ap[:, 0:2048].rearrange("p (k n) -> p k n", k=16)
                s3a = s3.ap[:, 0:2048].rearrange("p (k n) -> p k n", k=16)
                s3b = s3.ap[:, 2048:4096].rearrange("p (k n) -> p k n", k=16)
                t1, t2 = arf(5 + 2 * (m % 2)), arf(6 + 2 * (m % 2))
                tb = arbufs(5 + 2 * (m % 2), 2)
                pG1 = ps_next()
                for kc in range(16):
                    MM(pG1.ap[:, 0:TB], s3a[:, kc, :], hT_t[:, kc, 0:TB], kc == 0, kc == 15, [s3, hT], [pG1], kc == 15)
                pG2 = ps_next()
                for kc in range(16):
                    MM(pG2.ap[:, 0:TB], s3b[:, kc, :], hT_t[:, kc, 0:TB], kc == 0, kc == 15, [s3, hT], [pG2], kc == 15)
                W.done(s3)
                ACT(t1[:, 0:TB], pG1.ap[:, 0:TB], AF.Tanh, [pG1], tb, scale=0.5)
                ACT(t2[:, 0:TB], pG2.ap[:, 0:TB], AF.Tanh, [pG2], tb, scale=0.5)
                pP1 = ps_next()
                for kc in range(32):
                    MM(pP1.ap[:, 0:TB], s1v[:, kc, :], ysT_t[:, kc, 0:TB], kc == 0, kc == 31, [s1] + ysT, [pP1], kc == 31)
                W.done(s1)
                pP2 = ps_next()
                for kc in range(16):
                    MM(pP2.ap[:, 0:TB], s2v[:, kc, :], ygT_t[:, kc, 0:TB], kc == 0, kc == 15, [s2] + ygT, [pP2], kc == 15)
                W.done(s2)
                STT(t1[:, 0:TB], t1[:, 0:TB], 1.0, pP1.ap[:, 0:TB], ALU.add, ALU.mult, [pP1] + tb, tb)
                STT(t2[:, 0:TB], t2[:, 0:TB], 1.0, pP2.ap[:, 0:TB], ALU.add, ALU.mult, [pP2] + tb, tb)
                TT(mixT_t[:, m, 0:TB], t1[:, 0:TB], t2[:, 0:TB], ALU.add, tb, [mixT])
            if os.environ.get("KVERB"):
                print("phase_d final-proj start nops", S.nops)
            for cc in range(4):
                so = [W.get(("O", cc, i), lambda s_, cc=cc, i=i: load_w(s_.ap[:].rearrange("p (k n) -> p k n", k=16),
                                                                      wo_v[:, :, 512 * cc + 256 * i:512 * cc + 256 * i + 256], s_))
                      for i in range(2)]
                sov = [s_.ap[:].rearrange("p (k n) -> p k n", k=16) for s_ in so]
                for t in range(nt):
                    cols = slice(t * tn, (t + 1) * tn)
                    stage = ysT_t[:, 8 * t:8 * t + 8, :].rearrange("p a b -> p (a b)").bitcast(F32)
                    stb = ysT[2 * t:2 * t + 2]
                    pO = ps_next()
                    for i in range(2):
                        for m in range(16):
                            MM(pO.ap[0:tn, 256 * i:256 * i + 256], mixT_t[:, m, cols], sov[i][:, m, :], m == 0, m == 15, [mixT, so[i]], [pO],
                               m == 15, skip=True)
                    if t == nt - 1:
                        W.done(so[0])
                        W.done(so[1])
                    junk = arb(9)[:, 0:512]
                    CP(stage[0:tn, 512 * cc:512 * cc + 512], pO.ap[0:tn, :], [pO], stb, eng="dve")
                    ACT(junk[0:tn, :], stage[0:tn, 512 * cc:512 * cc + 512], AF.Square, stb, arbufs(9) + [small], scale=0.5,
                        accum=small.ap[0:tn, 8 + 2 * (4 * t + cc):9 + 2 * (4 * t + cc)])
            if os.environ.get("KVERB"):
                print("phase_d epilogue start nops", S.nops)
            for t in range(nt):
                stage = ysT_t[:, 8 * t:8 * t + 8, :].rearrange("p a b -> p (a b)").bitcast(F32)
                stb = ysT[2 * t:2 * t + 2]
                xr = ygT_t[:, 8 * (t % 2):8 * (t % 2) + 8, :].rearrange("p a b -> p (a b)").bitcast(F32)
                xrb = ygT[2 * (t % 2):2 * (t % 2) + 2]
                r0 = t0 + t * tn
                DMA("sp", xr[0:tn, :], x_in[seq][r0:r0 + tn, :], [], xrb, sem_buf=xrb[0])
                TT(small.ap[0:tn, 6:7], small.ap[0:tn, 8 + 8 * t:9 + 8 * t], small.ap[0:tn, 10 + 8 * t:11 + 8 * t], ALU.add, [small], [small])
                TT(small.ap[0:tn, 6:7], small.ap[0:tn, 6:7], small.ap[0:tn, 12 + 8 * t:13 + 8 * t], ALU.add, [small], [small])
                TT(small.ap[0:tn, 6:7], small.ap[0:tn, 6:7], small.ap[0:tn, 14 + 8 * t:15 + 8 * t], ALU.add, [small], [small])
                RSTD(small.ap[0:tn, 7:8], small.ap[0:tn, 6:7], 4.0 / D, 4e-6, [small])
                STT(stage[0:tn, :], stage[0:tn, :], small.ap[0:tn, 7:8], pg[0:tn, :], ALU.mult, ALU.mult, stb + [small] + pgb, stb)
                TT(stage[0:tn, :], stage[0:tn, :], xr[0:tn, :], ALU.add, stb + xrb, stb, eng="pool")
                DMA("sp", y_out[seq][r0:r0 + tn, :], stage[0:tn, :], stb, [], sem_buf=stb[0])

        def init_states(seq):
            if seq == "p":
                MEMSET(hist_t[:], 0.0, hist, eng="pool")
                MEMSET(ST_t[:], 0.0, ST, eng="pool")
                MEMSET(Sg_t[:], 0.0, Sg, eng="pool")
                return
            stg = arf(0, 12)
            sb_ = arbufs(0, 12)
            MEMSET(stg[0:4, :], 0.0, sb_, eng="pool")
            DMA("sp", stg[0:3, :], cs_in[:, :], [], sb_, sem_buf=sb_[0])
            pb = ps_next()
            for cg in range(48):
                TR(pb.ap[:, 4 * cg:4 * cg + 4], stg[0:4, cg * 128:(cg + 1) * 128], IDN[0:4, 0:4], sb_ + [cst], [pb], cg == 47)
            CP(hist_t[:], pb.ap[:, 0:192].rearrange("p (a b) -> p a b", b=4)[:, :, 0:3], [pb], hist)
            stg2 = arf(12, 8).rearrange("p (a b) -> p a b", a=32)
            sb2 = arbufs(12, 8)
            DMA("sp", stg2, ss_in.rearrange("(a p) n -> p a n", p=128), [], sb2, sem_buf=sb2[0])
            for q in range(8):
                pb = ps_next()
                for i in range(4):
                    TR(pb.ap[:, 128 * i:128 * i + 128], stg2[:, 4 * q + i, :], IDN, sb2 + [cst], [pb], i == 3)
                CP(ST[q].ap[:, :], pb.ap[:, :], [pb], [ST[q]])
            DMA("sp", Sg_t[:].rearrange("p (h c) v -> p h c v", h=4), gs_in.rearrange("h (c p) v -> p h c v", p=128), [], Sg, sem_buf=Sg[0])

        def final_states(seq):
            stg = arf(0, 12)
            sb_ = arbufs(0, 12)
            for r in range(3):
                pbs = [ps_next() for _ in range(4)]
                for j in range(16):
                    cg = 16 * r + j
                    pb = pbs[j // 4]
                    TR(pb.ap[0:3, 128 * (j % 4):128 * (j % 4) + 128], hist[cg].ap[:, 0:3], IDN, [hist[cg], cst], [pb], (j % 4) == 3)
                for q in range(4):
                    CP(stg[0:3, 2048 * r + 512 * q:2048 * r + 512 * q + 512], pbs[q].ap[0:3, :], [pbs[q]], sb_)
            DMA("sp", conv_out[seq][:, :], stg[0:3, :], sb_, [], sem_buf=sb_[0])
            stg2 = arf(12, 8).rearrange("p (a b) -> p a b", a=32)
            sb2 = arbufs(12, 8)
            for q in range(8):
                pb = ps_next()
                for i in range(4):
                    TR(pb.ap[:, 128 * i:128 * i + 128], ST[q].ap[:, 128 * i:128 * i + 128], IDN, [ST[q], cst], [pb], i == 3)
                CP(stg2[:, 4 * q:4 * q + 4, :], pb.ap[:, :].rearrange("p (a b) -> p a b", a=4), [pb], sb2)
            DMA("sp", ssd_out[seq].rearrange("(a p) n -> p a n", p=128), stg2, sb2, [], sem_buf=sb2[0])
            DMA("sp", gla_out[seq].rearrange("h (c p) v -> p h c v", p=128), Sg_t[:].rearrange("p (h c) v -> p h c v", h=4), Sg, [], sem_buf=Sg[0])

        def schedule():
            for seq, T in (("s", S_SEQ), ("p", P_SEQ)):
                if seq == "p" and "S" in dbg:
                    continue
                if "i" in dbg:
                    init_states(seq)
                if seq == "s":
                    blocks = [(0, 1, 64)]
                else:
                    nt = min(4, T // 128)
                    blocks = [(b * nt * 128, nt, 128) for b in range(T // (nt * 128))]
                for (t0, nt, tn) in blocks:
                    cur_u[0] = 0
                    if "a" in dbg:
                        phase_a(seq, t0, nt, tn)
                    phase_b(nt, tn)
                    if stages >= 2:
                        phase_c(nt, tn)
                    if stages >= 3:
                        phase_d(seq, t0, nt, tn)
                    cur_blk[0] += 1
                if "f" in dbg:
                    final_states(seq)

        S.dry = True
        W.record = True
        schedule()
        S.dry = False
        W.record = False
        ps_rr[0] = 0
        nblk = cur_blk[0]
        nu = max(e[3] for e in W.plan) + 1 if W.plan else 0
        same = all(W.plan[i][0] == W.plan[i % nu][0] and W.plan[i][3] == i % nu for i in range(len(W.plan))) and len(W.plan) == nu * nblk
        if nblk > 1 and same and not os.environ.get("KNOSCR"):
            wscr[0] = nc.dram_tensor("wscr", [nu, 128, 4096], BF16).ap()
            for sl_ in slots:
                MEMSET(sl_.ap[:, :], 0.0, [sl_], eng="pool")
        cur_blk[0] = 0
        schedule()
        S.wait_all("sp", AR + ysT + ygT + Sg + ST + [mixT])
        S.emit()
    return nc


def make_consts(inp):
    f = np.float32
    cst = np.zeros((128, K_END), f)
    cst[:, K_GPRE:K_GPRE + 16] = inp["norm_pre_gain"][0].reshape(16, 128).T
    cst[:, K_GSSD:K_GSSD + 32] = inp["ssd_norm_gain"][0].reshape(32, 128).T
    cst[:, K_GGLA:K_GGLA + 4] = inp["gla_norm_gain"][0].reshape(4, 128).T
    cw = inp["conv_w"][0]
    cst[:, K_CW:K_CW + 192] = cw.reshape(4, 48, 128).transpose(2, 1, 0).reshape(128, 192)
    cst[:, K_CB:K_CB + 48] = inp["conv_b"][0].reshape(48, 128).T
    cst[:, K_GKB:K_GKB + 8] = inp["gla_gk_b"][0].reshape(8, 128).T
    cst[:, K_DTB:K_DTB + 64] = np.broadcast_to(inp["dt_bias"][0], (128, 64))
    cst[:, K_ALOG:K_ALOG + 64] = np.broadcast_to(inp["a_log"][0], (128, 64))
    cst[:, K_DSK:K_DSK + 64] = np.broadcast_to(inp["d_skip"][0], (128, 64))
    cst[:, K_U:K_U + 128] = np.triu(np.ones((128, 128), f))
    cst[:, K_IDN:K_IDN + 128] = np.eye(128, dtype=f)
    cst[:, K_NEG:K_NEG + 128] = np.tril(np.full((128, 128), -30000.0, f), -1)
    cst[:, K_ONES:K_ONES + 128] = 1.0
    pg = np.ascontiguousarray(np.broadcast_to(inp["norm_post_gain"][0], (128, D))).astype(f)
    return cst, pg


_PROG = {}


def run(inp, P_SEQ, stages=3, n_cores=8, trace=False):
    key = (P_SEQ, stages)
    if key not in _PROG:
        _PROG[key] = build_program(P_SEQ, stages)
    nc = _PROG[key]
    cst, pg = make_consts(inp)
    shared = dict(
        w_in=np.ascontiguousarray(inp["w_in"][0]), wbs=np.ascontiguousarray(inp["w_branch_ssd"][0]),
        wbg=np.ascontiguousarray(inp["w_branch_gla"][0]), wo=np.ascontiguousarray(inp["w_out"][0]),
        gkw=np.ascontiguousarray(inp["gla_gk_w"][0]), cst=cst, pgrep=pg)
    in_maps = []
    for i in range(n_cores):
        m = dict(shared)
        m["xp"] = np.ascontiguousarray(inp["x_prompt"][i, :P_SEQ])
        m["xs"] = np.ascontiguousarray(inp["x_sample"][i])
        m["cs_in"] = np.ascontiguousarray(inp["state_conv_ssd"][0, i])
        m["ss_in"] = np.ascontiguousarray(inp["state_ssd"][0, i]).reshape(4096, 128)
        m["gs_in"] = np.ascontiguousarray(inp["state_gla"][0, i])
        in_maps.append(m)
    res = run_bass_kernel_spmd(nc, in_maps, core_ids=list(range(n_cores)), **({"trace": True} if trace else {}))
    R = res.results
    st = lambda k: np.stack([np.asarray(R[i][k]) for i in range(n_cores)])
    outs = (st("yp"), st("ys"),
            st("conv_p")[None], st("ssd_p").reshape(n_cores, 64, 64, 128)[None], st("gla_p")[None],
            st("conv_s")[None], st("ssd_s").reshape(n_cores, 64, 64, 128)[None], st("gla_s")[None])
    return outs, res


def kernel(**inputs):
    inp = {k: np.asarray(v) for k, v in inputs.items()}
    outs, _ = run(inp, P_SEQ_FULL, 3, 8)
    return tuple(np.ascontiguousarray(o, dtype=np.float32) for o in outs)
```
